# Optimizing a Trainium2 kernel written in Bass

```python
import math
import jax, jax.numpy as jnp
from jax import lax
import numpy as np

D_MODEL = 1024
BATCH = 4
SEQ = 4096
DEPTH = 4

HEAD_DIM = 64
FOURIER_WIDTH = D_MODEL // 4
FOURIER_CH = HEAD_DIM
N_FOURIER_GROUPS = FOURIER_WIDTH // FOURIER_CH
ATTN_WIDTH = D_MODEL // 2
N_Q_HEADS = ATTN_WIDTH // HEAD_DIM
N_KV_HEADS = 2
KV_WIDTH = N_KV_HEADS * HEAD_DIM
MEM_WIDTH = D_MODEL // 4
N_MEM_HEADS = MEM_WIDTH // HEAD_DIM
N_MEM = 256
MIX_WIDTH = FOURIER_WIDTH + ATTN_WIDTH + MEM_WIDTH
IN_WIDTH = FOURIER_WIDTH + ATTN_WIDTH + 2 * KV_WIDTH + MEM_WIDTH
WINDOW = 128
BLOCK = 128
ROPE_THETA = 10000.0
D_FF = -(-8 * D_MODEL // (3 * 256)) * 256
EPS = 1e-6
NEG_INF = -1e30

kernel_name = 'hybrid_fourier_window_memory_encoder'


def _rmsnorm(x, g):
    xf = x.astype(jnp.float32)
    y = xf * lax.rsqrt(jnp.mean(xf * xf, axis=-1, keepdims=True) + EPS)
    return (y * g.astype(jnp.float32)).astype(x.dtype)


def _rope(t, positions):
    half = HEAD_DIM // 2
    inv_freq = jnp.exp(-math.log(ROPE_THETA) * jnp.arange(half, dtype=jnp.float32) * (2.0 / HEAD_DIM))
    ang = positions.astype(jnp.float32)[:, :, None] * inv_freq
    cos = jnp.cos(ang)[:, :, None, :]
    sin = jnp.sin(ang)[:, :, None, :]
    tf = t.astype(jnp.float32)
    t1, t2 = tf[..., :half], tf[..., half:]
    return jnp.concatenate([t1 * cos - t2 * sin, t2 * cos + t1 * sin], axis=-1).astype(t.dtype)


def _fourier_mix(u, w_f):
    b, s, _ = u.shape
    ug = u.reshape(b, s, N_FOURIER_GROUPS, FOURIER_CH).astype(jnp.float32)
    spec = jnp.fft.fftn(ug, axes=(1, 3), norm='ortho').real
    y = jnp.einsum('bsgc,gce->bsge', spec.astype(u.dtype), w_f)
    return y.reshape(b, s, FOURIER_WIDTH)


def _window_attn(q, k, v, sink):
    b, s, _, _ = q.shape
    nb = s // BLOCK
    grp = N_Q_HEADS // N_KV_HEADS

    def bands(t):
        tp = jnp.pad(t, ((0, 0), (BLOCK, BLOCK), (0, 0), (0, 0)))
        tp = tp.reshape(b, nb + 2, BLOCK, N_KV_HEADS, HEAD_DIM)
        return jnp.concatenate([tp[:, :-2], tp[:, 1:-1], tp[:, 2:]], axis=2)

    kb, vb = bands(k), bands(v)
    qb = q.reshape(b, nb, BLOCK, N_KV_HEADS, grp, HEAD_DIM)
    scores = jnp.einsum('bnqhgd,bnjhd->bnhgqj', qb, kb).astype(jnp.float32) * (HEAD_DIM ** -0.5)
    blk = jnp.arange(nb)[:, None, None] * BLOCK
    qpos = blk + jnp.arange(BLOCK)[None, :, None]
    kpos = blk - BLOCK + jnp.arange(3 * BLOCK)[None, None, :]
    valid = (kpos >= 0) & (kpos < s) & (jnp.abs(qpos - kpos) <= WINDOW)
    scores = jnp.where(valid[None, :, None, None], scores, NEG_INF)
    sink_b = jnp.broadcast_to(sink.astype(jnp.float32).reshape(1, 1, N_KV_HEADS, grp, 1, 1),
                              scores.shape[:-1] + (1,))
    probs = jax.nn.softmax(jnp.concatenate([scores, sink_b], axis=-1), axis=-1)[..., :-1]
    out = jnp.einsum('bnhgqj,bnjhd->bnqhgd', probs.astype(v.dtype), vb)
    return out.reshape(b, s, N_Q_HEADS * HEAD_DIM)


def _memory_attn(q, km, vm):
    b, s, _, _ = q.shape
    scores = jnp.einsum('bshd,bmhd->bhsm', q, km).astype(jnp.float32) * (HEAD_DIM ** -0.5)
    probs = jax.nn.softmax(scores, axis=-1)
    out = jnp.einsum('bhsm,bmhd->bshd', probs.astype(vm.dtype), vm)
    return out.reshape(b, s, MEM_WIDTH)


def setup_inputs(seed: int = 0) -> dict:
    key = jax.random.key(seed)
    ks = jax.random.split(key, 20)
    f32 = jnp.float32

    def nrm(k, shape, scale):
        return jax.random.normal(k, shape, f32) * scale

    def gain(k, shape):
        return 1.0 + 0.05 * jax.random.normal(k, shape, f32)

    x = jax.random.normal(ks[0], (BATCH, SEQ, D_MODEL), f32)
    mem = jax.random.normal(ks[1], (BATCH, N_MEM, D_MODEL), f32)
    positions = jnp.broadcast_to(jnp.arange(SEQ, dtype=jnp.int32)[None, :], (BATCH, SEQ))
    return {
        'x': x,
        'mem': mem,
        'positions': positions,
        'g_pre_mix': gain(ks[2], (DEPTH, D_MODEL)),
        'w_in': nrm(ks[3], (DEPTH, D_MODEL, IN_WIDTH), D_MODEL ** -0.5),
        'w_fourier': nrm(ks[4], (DEPTH, N_FOURIER_GROUPS, FOURIER_CH, FOURIER_CH), FOURIER_CH ** -0.5),
        'sink': nrm(ks[5], (DEPTH, N_Q_HEADS), 0.5),
        'g_mem': gain(ks[6], (DEPTH, D_MODEL)),
        'w_mem_kv': nrm(ks[7], (DEPTH, D_MODEL, 2 * MEM_WIDTH), D_MODEL ** -0.5),
        'g_grp': gain(ks[8], (DEPTH, MIX_WIDTH)),
        'w_out': nrm(ks[9], (DEPTH, MIX_WIDTH, D_MODEL), MIX_WIDTH ** -0.5),
        'g_post_mix': gain(ks[10], (DEPTH, D_MODEL)),
        'g_pre_ffn': gain(ks[11], (DEPTH, D_MODEL)),
        'w_ffn_in': nrm(ks[12], (DEPTH, D_MODEL, 2 * D_FF), D_MODEL ** -0.5),
        'w_ffn_out': nrm(ks[13], (DEPTH, D_FF, D_MODEL), D_FF ** -0.5),
        'g_post_ffn': gain(ks[14], (DEPTH, D_MODEL)),
    }


def reference(x, mem, positions, g_pre_mix, w_in, w_fourier, sink, g_mem, w_mem_kv, g_grp,
              w_out, g_post_mix, g_pre_ffn, w_ffn_in, w_ffn_out, g_post_ffn):
    b, s, _ = x.shape
    m = mem.shape[1]
    splits = [FOURIER_WIDTH,
              FOURIER_WIDTH + ATTN_WIDTH,
              FOURIER_WIDTH + ATTN_WIDTH + KV_WIDTH,
              FOURIER_WIDTH + ATTN_WIDTH + 2 * KV_WIDTH]
    for l in range(DEPTH):
        h = _rmsnorm(x, g_pre_mix[l])
        z = h @ w_in[l]
        zf, zq, zk, zv, zm = jnp.split(z, splits, axis=-1)

        y_f = _fourier_mix(zf, w_fourier[l])

        q = _rope(zq.reshape(b, s, N_Q_HEADS, HEAD_DIM), positions)
        k = _rope(zk.reshape(b, s, N_KV_HEADS, HEAD_DIM), positions)
        v = zv.reshape(b, s, N_KV_HEADS, HEAD_DIM)
        y_a = _window_attn(q, k, v, sink[l])

        mkv = _rmsnorm(mem, g_mem[l]) @ w_mem_kv[l]
        km, vm = jnp.split(mkv, 2, axis=-1)
        km = km.reshape(b, m, N_MEM_HEADS, HEAD_DIM)
        vm = vm.reshape(b, m, N_MEM_HEADS, HEAD_DIM)
        y_m = _memory_attn(zm.reshape(b, s, N_MEM_HEADS, HEAD_DIM), km, vm)

        gg = g_grp[l]
        y = jnp.concatenate([
            _rmsnorm(y_f, gg[:FOURIER_WIDTH]),
            _rmsnorm(y_a, gg[FOURIER_WIDTH:FOURIER_WIDTH + ATTN_WIDTH]),
            _rmsnorm(y_m, gg[FOURIER_WIDTH + ATTN_WIDTH:]),
        ], axis=-1) @ w_out[l]
        x = x + _rmsnorm(y, g_post_mix[l])

        h = _rmsnorm(x, g_pre_ffn[l])
        gate, up = jnp.split(h @ w_ffn_in[l], 2, axis=-1)
        f = (jax.nn.silu(gate) * up) @ w_ffn_out[l]
        x = x + _rmsnorm(f, g_post_ffn[l])
    return x
```

```python
import contextlib
import numpy as np
import ml_dtypes
import concourse.bass as bass
import concourse.mybir as mybir
from concourse.bass_utils import run_bass_kernel_spmd

F32 = mybir.dt.float32
BF16 = mybir.dt.bfloat16
I32 = mybir.dt.int32
AF = mybir.ActivationFunctionType
ALU = mybir.AluOpType
NPBF = ml_dtypes.bfloat16

D = 1024
S = 4096
NT = 2048
DFF = 2816
NF = 22
EPS = 1e-6
TWO_PI = 2.0 * np.pi


class Tok:
    __slots__ = ("sem", "val")

    def __init__(self, sem, val):
        self.sem = sem
        self.val = val


class Buf:
    __slots__ = ("w", "r", "slot")

    def __init__(self):
        self.w = {}
        self.r = {}
        self.slot = None


def _merge(d, tok):
    k = id(tok.sem)
    if k not in d or d[k].val < tok.val:
        d[k] = tok


class Slot:
    def __init__(self, prog, name="sl"):
        self.sem = prog.new_sem(name)
        self.cnt = 0

    def tok(self):
        return Tok(self.sem, self.cnt)


class Prog:
    ENG = ("pe", "act", "dve", "pool", "sp")

    def __init__(self, nc, es):
        self.nc = nc
        self.es = es
        self.q = {e: [] for e in self.ENG}
        self.nsem = 0
        self.sem = {e: self.new_sem("prog_" + e) for e in self.ENG}
        self.cnt = {e: 0 for e in self.ENG}
        self.waited = {}

    def new_sem(self, name):
        self.nsem += 1
        return self.es.enter_context(self.nc.semaphore(f"{name}_{self.nsem}"))

    def wait(self, eng, tok):
        key = (eng, id(tok.sem))
        if self.waited.get(key, 0) >= tok.val:
            return
        self.waited[key] = tok.val
        sem, val = tok.sem, tok.val
        self.q[eng].append(lambda e: e.wait_ge(sem, val))

    def _deps(self, eng, reads, writes):
        for b in reads:
            for t in b.w.values():
                self.wait(eng, t)
        for b in writes:
            for t in b.w.values():
                self.wait(eng, t)
            for t in b.r.values():
                self.wait(eng, t)

    def _note(self, tok, reads, writes):
        for b in reads:
            _merge(b.r, tok)
        for b in writes:
            b.w = {id(tok.sem): tok}
            b.r = {}

    def op(self, eng, fn, reads=(), writes=(), sig=True):
        self._deps(eng, reads, writes)
        if not sig:
            self.q[eng].append(fn)
            return None
        self.cnt[eng] += 1
        sem = self.sem[eng]
        self.q[eng].append(lambda e: fn(e).then_inc(sem, 1))
        tok = Tok(sem, self.cnt[eng])
        self._note(tok, reads, writes)
        return tok

    def dma(self, eng, out, in_, reads=(), writes=(), slot=None, **kw):
        self._deps(eng, reads, writes)
        if slot is None:
            owner = (list(writes) + list(reads))[0]
            if owner.slot is None:
                owner.slot = Slot(self, "d")
            slot = owner.slot
        slot.cnt += 16
        sem = slot.sem
        self.q[eng].append(lambda e: e.dma_start(out=out, in_=in_, **kw).then_inc(sem, 16))
        tok = slot.tok()
        self._note(tok, reads, writes)
        return tok

    def mm(self, out, lhsT, rhs, start, stop, reads=(), writes=(), sig=False):
        return self.op("pe", lambda e: e.matmul(out, lhsT, rhs, start=start, stop=stop), reads, writes, sig)

    def tr(self, out, in_, ident, reads=(), writes=(), sig=True):
        return self.op("pe", lambda e: e.transpose(out, in_, ident), reads, writes, sig)

    def act(self, out, in_, func, reads=(), writes=(), **kw):
        return self.op("act", lambda e: e.activation(out=out, in_=in_, func=func, **kw), reads, writes)

    def tt(self, eng, out, in0, in1, op, reads=(), writes=()):
        return self.op(eng, lambda e: e.tensor_tensor(out=out, in0=in0, in1=in1, op=op), reads, writes)

    def ts(self, eng, out, in0, s1, s2, op0, op1=None, reads=(), writes=()):
        if op1 is None:
            return self.op(eng, lambda e: e.tensor_scalar(out=out, in0=in0, scalar1=s1, scalar2=None, op0=op0), reads, writes)
        return self.op(eng, lambda e: e.tensor_scalar(out=out, in0=in0, scalar1=s1, scalar2=s2, op0=op0, op1=op1), reads, writes)

    def stt(self, out, in0, scalar, in1, op0, op1, reads=(), writes=()):
        return self.op("dve", lambda e: e.scalar_tensor_tensor(out=out, in0=in0, scalar=scalar, in1=in1, op0=op0, op1=op1), reads, writes)

    def copy(self, eng, out, in_, reads=(), writes=()):
        return self.op(eng, lambda e: e.tensor_copy(out=out, in_=in_), reads, writes)

    def recip(self, out, in_, reads=(), writes=()):
        return self.op("dve", lambda e: e.reciprocal(out=out, in_=in_), reads, writes)

    def memset(self, eng, ap, val, writes=()):
        return self.op(eng, lambda e: e.memset(ap, val), (), writes)

    def barrier(self):
        nc = self.nc
        q = self.q
        self.q = {e: [] for e in self.ENG}
        with nc.Block() as block:
            @block.tensor
            def _(eng):
                for f in q["pe"]:
                    f(eng)

            @block.scalar
            def _(eng):
                for f in q["act"]:
                    f(eng)

            @block.vector
            def _(eng):
                for f in q["dve"]:
                    f(eng)

            @block.gpsimd
            def _(eng):
                for f in q["pool"]:
                    f(eng)

            @block.sync
            def _(eng):
                for f in q["sp"]:
                    f(eng)


class Ring:
    def __init__(self, items):
        self.items = items
        self.bufs = [Buf() for _ in items]
        self.i = 0

    def get(self):
        i = self.i
        self.i = (i + 1) % len(self.items)
        return self.items[i], self.bufs[i]


def build(mode):
    nc = bass.Bass("TRN2", target_bir_lowering=False)

    def din(name, shape, dt):
        return nc.dram_tensor(name, list(shape), dt, kind="ExternalInput").ap()

    def dout(name, shape, dt):
        return nc.dram_tensor(name, list(shape), dt, kind="ExternalOutput").ap()

    xT = din("xT", [D, NT], F32)
    pos = din("pos", [1, NT], I32)
    w_in_t = din("w_in_t", [15, 128, 1024], F32)
    gains = din("gains", [128, 40], F32)
    cst = din("cst", [128, 8], F32)
    cs_mat = din("cs_mat", [128, 256], F32)
    if mode == "A":
        pq_o = dout("pq_o", [NT, 512], BF16)
        kh_o = dout("kh_o", [128, 256], BF16)
        vh_o = dout("vh_o", [2, 128, 128], BF16)
    else:
        memT = din("memT", [D, 256], F32)
        ident_d = din("ident", [128, 128], F32)
        masks_d = din("masks", [4, 128, 128], F32)
        pq_all = din("pq_all", [S, 512], BF16)
        kh_in = din("kh_in", [128, 256], BF16)
        vh_in = din("vh_in", [2, 128, 128], BF16)
        fm = din("fm", [2, 4, 8, 128, 2048], BF16)
        w_mem = din("w_mem", [D, 512], F32)
        wf = din("wf", [4, 64, 64], F32)
        sink = din("sink", [1, 8], F32)
        ggrp = din("ggrp", [1, D], F32)
        w_out_t = din("w_out_t", [8, 128, 1024], F32)
        w_ffn_in_t = din("w_ffn_in_t", [NF, 128, 2048], F32)
        w_ffn_out_t = din("w_ffn_out_t", [8, 128, DFF], F32)
        xT_out = dout("xT_out", [D, NT], F32)

    with contextlib.ExitStack() as es:
        P = Prog(nc, es)

        uid = [0]

        def sbt(st, name, shape, dt):
            uid[0] += 1
            return st.enter_context(nc.sbuf_tensor(f"s{uid[0]}_{name}", list(shape), dt))

        def pst(st, name, shape, dt):
            uid[0] += 1
            return st.enter_context(nc.psum_tensor(f"p{uid[0]}_{name}", list(shape), dt))

        def bcast(ap, n):
            return bass.AP(ap.tensor, ap.offset, [[0, 128], [1, n]])

        x_sb = sbt(es, "x_sb", [128, 8, NT], F32)
        X = [[Buf() for _ in range(4)] for _ in range(8)]
        cosT = sbt(es, "cosT", [128, NT], F32)
        sinT = sbt(es, "sinT", [128, NT], F32)
        ROPE = Buf()
        cst_sb = sbt(es, "cst_sb", [128, 8], F32)
        gains_sb = sbt(es, "gains_sb", [128, 40], F32)
        CST = Buf()
        ones_bf = sbt(es, "ones_bf", [128, 128], BF16)
        cs_bf = sbt(es, "cs_bf", [128, 256], BF16)
        ident_bf = sbt(es, "ident_bf", [128, 128], BF16)
        v_sb = sbt(es, "v_sb", [128, 18, 2, 65], BF16)
        V = [Buf() for _ in range(18)]
        VONES = Buf()

        for c in range(8):
            P.dma("sp", x_sb[:, c, :], xT[c * 128:(c + 1) * 128, :], writes=X[c])
        P.dma("sp", cst_sb[:], cst, writes=[CST])
        P.dma("sp", gains_sb[:], gains, writes=[CST])
        P.dma("pool", cs_bf[:], cs_mat, writes=[CST])
        if mode == "B":
            P.dma("pool", ident_bf[:], ident_d, writes=[CST])
        P.memset("dve", ones_bf[:], 1.0 / D, writes=[CST])
        P.memset("dve", v_sb[:, :, :, 64:65], 1.0, writes=[VONES])
        eps_ap = cst_sb[:, 3:4]

        with contextlib.ExitStack() as s0:
            pos_i = sbt(s0, "pos_i", [128, NT], I32)
            ang = sbt(s0, "ang", [128, NT], F32)
            a2 = sbt(s0, "a2", [128, NT], F32)
            n_i = sbt(s0, "n_i", [128, NT], I32)
            n_f = sbt(s0, "n_f", [128, NT], F32)
            B_pos, B_ang, B_a2, B_ni, B_nf = Buf(), Buf(), Buf(), Buf(), Buf()
            P.dma("sp", pos_i[:], bcast(pos, NT), writes=[B_pos])
            P.copy("dve", ang[:], pos_i[:], reads=[B_pos], writes=[B_ang])
            P.ts("dve", ang[:], ang[:], cst_sb[:, 0:1], None, ALU.mult, reads=[CST], writes=[B_ang])
            for tab, shift, scale in ((sinT, 0.0, cst_sb[:, 1:2]), (cosT, 0.5 * np.pi, 1.0)):
                P.ts("dve", a2[:], ang[:], shift, 1.0 / TWO_PI, ALU.add, ALU.mult, reads=[B_ang], writes=[B_a2])
                P.copy("dve", n_i[:], a2[:], reads=[B_a2], writes=[B_ni])
                P.copy("dve", n_f[:], n_i[:], reads=[B_ni], writes=[B_nf])
                P.ts("dve", a2[:], ang[:], shift, None, ALU.add, reads=[B_ang], writes=[B_a2])
                P.stt(a2[:], n_f[:], -TWO_PI, a2[:], ALU.mult, ALU.add, reads=[B_nf], writes=[B_a2])
                P.ts("dve", a2[:], a2[:], -3.1415925, 3.1415925, ALU.max, ALU.min, writes=[B_a2])
                P.act(tab[:], a2[:], AF.Sin, reads=[B_a2, CST], writes=[ROPE], scale=scale)
            P.barrier()

        def prenorm(tg, gidx, sq_ring, rstd_ring, st_ring, h_ap_fn, HB):
            sl = slice(tg * 512, (tg + 1) * 512)
            sq, SQ = sq_ring.get()
            SQc = [Buf() for _ in range(8)]
            for c in range(8):
                P.act(sq[:, c, :], x_sb[:, c, sl], AF.Square, reads=[X[c][tg]], writes=[SQc[c]] + ([SQ] if c == 0 else []))
            bank, BK = st_ring.get()
            for c in range(8):
                last = c == 7
                P.mm(bank[:], ones_bf[:], sq[:, c, :], c == 0, last, reads=[SQc[c], CST] + ([SQ] if last else []),
                     writes=[BK], sig=last)
            rstd, RS = rstd_ring.get()
            P.act(rstd[:], bank[:], AF.Sqrt, reads=[BK, CST], writes=[RS], bias=eps_ap, scale=1.0)
            P.recip(rstd[:], rstd[:], writes=[RS])
            for c in range(8):
                P.stt(h_ap_fn(c), x_sb[:, c, sl], gains_sb[:, gidx * 8 + c:gidx * 8 + c + 1], rstd[:],
                      ALU.mult, ALU.mult, reads=[X[c][tg], RS, CST], writes=[HB[c]])

        def postnorm_update(tg, gidx, yo, YO, rstd_ring, bank, BK):
            sl = slice(tg * 512, (tg + 1) * 512)
            rstd, RS = rstd_ring.get()
            P.act(rstd[:], bank[:], AF.Sqrt, reads=[BK, CST], writes=[RS], bias=eps_ap, scale=1.0)
            P.recip(rstd[:], rstd[:], writes=[RS])
            for c in range(8):
                P.tt("dve", yo[:, c, :], yo[:, c, :], rstd[:], ALU.mult, reads=[RS], writes=[YO[c]])
                P.stt(x_sb[:, c, sl], yo[:, c, :], gains_sb[:, gidx * 8 + c:gidx * 8 + c + 1], x_sb[:, c, sl],
                      ALU.mult, ALU.add, reads=[YO[c], CST], writes=[X[c][tg]])

        with contextlib.ExitStack() as sm:
            bufA = sbt(sm, "bufA", [128, 8, NT], BF16)
            HT = [[Buf() for _ in range(4)] for _ in range(8)]
            YC = [[Buf() for _ in range(16)] for _ in range(8)]
            qT = sbt(sm, "qT", [128, 4, NT], BF16)
            Q = [[Buf() for _ in range(4)] for _ in range(4)]
            kT = sbt(sm, "kT", [128, 18 * 128], BF16)
            KTG = [Buf() for _ in range(4)]
            KH = [Buf(), Buf()]
            zmT = sbt(sm, "zmT", [128, 2, NT], BF16)
            ZM = [[Buf() for _ in range(4)] for _ in range(2)]

            def KB(bt):
                return KH[0] if bt == 0 else (KH[1] if bt == 17 else KTG[(bt - 1) // 4])

            with contextlib.ExitStack() as s1:
                sq_ring = Ring([sbt(s1, f"sq{i}", [128, 8, 512], BF16) for i in range(2)])
                rstd_ring = Ring([sbt(s1, f"rstd{i}", [128, 512], F32) for i in range(2)])
                wring = Ring([sbt(s1, f"wr{i}", [128, 8, 128], BF16) for i in range(4)])
                zfT = sbt(s1, "zfT", [128, 2, NT], BF16)
                ZF = [[Buf() for _ in range(4)] for _ in range(2)]
                rt1 = Ring([sbt(s1, f"rt1_{i}", [128, 512], F32) for i in range(2)])
                rt2 = Ring([sbt(s1, f"rt2_{i}", [128, 512], F32) for i in range(2)])
                pqst = Ring([sbt(s1, f"pqst{i}", [128, 512], BF16) for i in range(2)])
                gen = Ring([pst(s1, f"g1_{i}", [128, 512], F32) for i in range(6)])
                stb = Ring([pst(s1, f"s1_{i}", [128, 512], F32) for i in range(2)])

                for tg in range(4):
                    prenorm(tg, 0, sq_ring, rstd_ring, stb,
                            lambda c, tg=tg: bufA[:, c, tg * 512:(tg + 1) * 512], [HT[c][tg] for c in range(8)])

                loads = [0, 1, 2, 7, 3, 8, 4, 9, 5, 10, 6, 11, 12, 13, 14]
                loaded = {}
                nxt = [0]

                def need(k):
                    while nxt[0] < len(loads) and nxt[0] <= k + 2:
                        j = nxt[0]
                        t, B = wring.get()
                        P.dma("pool", t[:], w_in_t[loads[j]].rearrange("p (c n) -> p c n", c=8), writes=[B])
                        loaded[j] = (t, B)
                        nxt[0] += 1
                    return loaded[k]

                def zmm(wt, WB, tg):
                    bank, BK = gen.get()
                    sl = slice(tg * 512, (tg + 1) * 512)
                    for c in range(8):
                        P.mm(bank[:], wt[:, c, :], bufA[:, c, sl], c == 0, c == 7,
                             reads=[WB, HT[c][tg]], writes=[BK], sig=(c == 7))
                    return bank, BK

                li = 0
                for m in range(2):
                    wt, WB = need(li); li += 1
                    for tg in range(4):
                        bank, BK = zmm(wt, WB, tg)
                        P.act(zfT[:, m, tg * 512:(tg + 1) * 512], bank[:], AF.Copy, reads=[BK], writes=[ZF[m][tg]])
                for qc in range(5):
                    wt, WB = need(li); wr_, WRB = need(li + 1); li += 2
                    for tg in range(4):
                        sl = slice(tg * 512, (tg + 1) * 512)
                        b1, BK1 = zmm(wt, WB, tg)
                        b2, BK2 = zmm(wr_, WRB, tg)
                        t1, T1 = rt1.get()
                        t2, T2 = rt2.get()
                        P.tt("dve", t1[:], b1[:], cosT[:, sl], ALU.mult, reads=[BK1, ROPE], writes=[T1])
                        P.tt("dve", t2[:], b2[:], sinT[:, sl], ALU.mult, reads=[BK2, ROPE], writes=[T2])
                        if qc < 4:
                            P.tt("dve", qT[:, qc, sl], t1[:], t2[:], ALU.add, reads=[T1, T2], writes=[Q[qc][tg]])
                        else:
                            P.tt("dve", kT[:, 128 + tg * 512:128 + (tg + 1) * 512], t1[:], t2[:], ALU.add,
                                 reads=[T1, T2], writes=[KTG[tg]])
                for m in range(2):
                    wt, WB = need(li); li += 1
                    for tg in range(4):
                        bank, BK = zmm(wt, WB, tg)
                        P.act(zmT[:, m, tg * 512:(tg + 1) * 512], bank[:], AF.Copy, reads=[BK], writes=[ZM[m][tg]])
                wt, WB = need(li); li += 1
                for tt in range(16):
                    bank, BK = gen.get()
                    for c in range(8):
                        P.mm(bank[:, 0:128], bufA[:, c, tt * 128:(tt + 1) * 128], wt[:, c, :], c == 0, c == 7,
                             reads=[WB, HT[c][tt // 4]], writes=[BK], sig=(c == 7))
                    P.act(v_sb[:, 1 + tt, :, 0:64], bank[:, 0:128].rearrange("p (h d) -> p h d", h=2), AF.Copy,
                          reads=[BK], writes=[V[1 + tt]])
                if mode == "A":
                    for tt in range(16):
                        bank, BK = gen.get()
                        for m in range(2):
                            P.mm(bank[:, m * 256:(m + 1) * 256], zfT[:, m, tt * 128:(tt + 1) * 128], cs_bf[:], True, True,
                                 reads=[ZF[m][tt // 4], CST], writes=[BK], sig=(m == 1))
                        stg, SB = pqst.get()
                        P.act(stg[:], bank[:], AF.Copy, reads=[BK], writes=[SB])
                        P.dma("sp", pq_o[tt * 128:(tt + 1) * 128, :], stg[:], reads=[SB])
                    KO, VO = Buf(), Buf()
                    P.dma("sp", kh_o[:, 0:128], kT[:, 128:256], reads=[KTG[0], KO])
                    P.dma("sp", kh_o[:, 128:256], kT[:, 16 * 128:17 * 128], reads=[KTG[3], KO])
                    P.dma("sp", vh_o[0].rearrange("p (h d) -> p h d", h=2), v_sb[:, 1, :, 0:64], reads=[V[1], VO])
                    P.dma("sp", vh_o[1].rearrange("p (h d) -> p h d", h=2), v_sb[:, 16, :, 0:64], reads=[V[16], VO])
                    for b_ in list(pqst.bufs) + [KO, VO]:
                        for t in b_.r.values():
                            P.wait("sp", t)
                P.barrier()

            if mode == "B":
                with contextlib.ExitStack() as s2:
                    memx = sbt(s2, "memx", [128, 8, 256], F32)
                    memh = sbt(s2, "memh", [128, 8, 256], BF16)
                    msq = sbt(s2, "msq", [128, 8, 256], BF16)
                    mrs = sbt(s2, "mrs", [128, 256], F32)
                    wm = sbt(s2, "wm", [128, 8, 512], BF16)
                    kmT = sbt(s2, "kmT", [128, 2, 256], BF16)
                    vm = sbt(s2, "vm", [128, 2, 4, 65], BF16)
                    masks = sbt(s2, "masks", [128, 4, 128], BF16)
                    esink = sbt(s2, "esink", [128, 8], F32)
                    ggb = sbt(s2, "ggb", [128, D], F32)
                    pT = Ring([sbt(s2, f"pT{i}", [128, 512], BF16) for i in range(6)])
                    pmT = [[sbt(s2, f"pmT{h}_{mc}", [128, 512], BF16) for mc in range(2)] for h in range(4)]
                    PM = [[Buf() for _ in range(2)] for _ in range(4)]
                    ya_ring = Ring([sbt(s2, f"ya{i}", [128, 512], F32) for i in range(2)])
                    yan_ring = Ring([sbt(s2, f"yan{i}", [128, 512], BF16) for i in range(2)])
                    junk = sbt(s2, "junk", [128, 512], BF16)
                    JK = Buf()
                    small = Ring([sbt(s2, f"small{i}", [128, 16], F32) for i in range(4)])
                    sbank = Ring([pst(s2, f"sb{i}", [128, 512], F32) for i in range(4)])
                    obank = Ring([pst(s2, f"ob{i}", [128, 512], F32) for i in range(2)])
                    tbank = Ring([pst(s2, f"tb{i}", [128, 1024], BF16) for i in range(2)])
                    MX, MH, MSQ, MRS, WM, KM, VM, MK, ES, GG = (Buf() for _ in range(10))

                    P.dma("sp", memx[:], memT.rearrange("(c p) m -> p c m", p=128), writes=[MX])
                    P.dma("pool", wm[:], w_mem.rearrange("(c p) n -> p c n", p=128), writes=[WM])
                    P.dma("pool", masks[:], masks_d.rearrange("k p q -> p k q"), writes=[MK])
                    P.dma("sp", esink[:], bcast(sink, 8), writes=[ES])
                    P.dma("sp", ggb[:], bcast(ggrp, D), writes=[GG])
                    P.dma("sp", kT[:, 0:128], kh_in[:, 0:128], writes=[KH[0]])
                    P.dma("sp", kT[:, 17 * 128:18 * 128], kh_in[:, 128:256], writes=[KH[1]])
                    P.dma("sp", v_sb[:, 0, :, 0:64], vh_in[0].rearrange("p (h d) -> p h d", h=2), writes=[V[0]])
                    P.dma("sp", v_sb[:, 17, :, 0:64], vh_in[1].rearrange("p (h d) -> p h d", h=2), writes=[V[17]])
                    P.act(esink[:], esink[:], AF.Exp, writes=[ES])
                    P.memset("dve", vm[:, :, :, 64:65], 1.0, writes=[VM])
                    for c in range(8):
                        P.act(msq[:, c, :], memx[:, c, :], AF.Square, reads=[MX], writes=[MSQ])
                    bank, BK = sbank.get()
                    for c in range(8):
                        P.mm(bank[:, 0:256], ones_bf[:], msq[:, c, :], c == 0, c == 7, reads=[MSQ, CST], writes=[BK], sig=(c == 7))
                    P.act(mrs[:], bank[:, 0:256], AF.Sqrt, reads=[BK, CST], writes=[MRS], bias=eps_ap, scale=1.0)
                    P.recip(mrs[:], mrs[:], writes=[MRS])
                    for c in range(8):
                        P.stt(memh[:, c, :], memx[:, c, :], gains_sb[:, 32 + c:33 + c], mrs[:], ALU.mult, ALU.mult,
                              reads=[MX, MRS, CST], writes=[MH])
                    MHr = [MH]
                    for j in range(2):
                        bank, BK = sbank.get()
                        for c in range(8):
                            P.mm(bank[:, 0:256], wm[:, c, j * 128:(j + 1) * 128], memh[:, c, :], c == 0, c == 7,
                                 reads=[WM] + MHr, writes=[BK], sig=(c == 7))
                        P.act(kmT[:, j, :], bank[:, 0:256], AF.Copy, reads=[BK], writes=[KM])
                    for mt in range(2):
                        bank, BK = sbank.get()
                        for c in range(8):
                            P.mm(bank[:, 0:256], memh[:, c, mt * 128:(mt + 1) * 128], wm[:, c, 256:512], c == 0, c == 7,
                                 reads=[WM] + MHr, writes=[BK], sig=(c == 7))
                        P.act(vm[:, mt, :, 0:64], bank[:, 0:256].rearrange("p (h d) -> p h d", h=4), AF.Copy,
                              reads=[BK], writes=[VM])

                    def groupnorm_T(y_ap, YB, width, col0, chunk0, tile):
                        sm_, SM = small.get()
                        P.act(junk[:, 0:width], y_ap, AF.Square, reads=[YB], writes=[JK, SM], accum_out=sm_[:, 0:1])
                        P.act(sm_[:, 1:2], sm_[:, 0:1], AF.Sqrt, reads=[CST], writes=[SM], bias=eps_ap, scale=1.0 / width)
                        P.recip(sm_[:, 1:2], sm_[:, 1:2], writes=[SM])
                        yn, YN = yan_ring.get()
                        P.stt(yn[:, 0:width], y_ap, sm_[:, 1:2], ggb[:, col0:col0 + width], ALU.mult, ALU.mult,
                              reads=[YB, SM, GG], writes=[YN])
                        nch = width // 128
                        tb, TB = tbank.get()
                        for j in range(nch):
                            P.tr(tb[:, j * 128:(j + 1) * 128], yn[:, j * 128:(j + 1) * 128], ident_bf[:],
                                 reads=[YN, CST], writes=[TB], sig=(j == nch - 1))
                        P.copy("act" if False else "dve", bufA[:, chunk0:chunk0 + nch, tile * 128:(tile + 1) * 128],
                               tb[:, 0:width].rearrange("p (j t) -> p j t", j=nch), reads=[TB],
                               writes=[YC[chunk0 + j][tile] for j in range(nch)])

                    for n in range(16):
                        ya, YA = ya_ring.get()
                        for h in range(2):
                            hp = slice(64 * h, 64 * h + 64)
                            pts = []
                            for mi in range(3):
                                bt = n + mi
                                bank, BK = sbank.get()
                                P.mm(bank[:], kT[hp, bt * 128:(bt + 1) * 128], qT[hp, :, n * 128:(n + 1) * 128], True, True,
                                     reads=[KB(bt)] + [Q[c][n // 4] for c in range(4)], writes=[BK], sig=True)
                                pt, PT = pT.get()
                                P.act(pt[:], bank[:], AF.Exp, reads=[BK], writes=[PT], scale=0.125)
                                if mi != 1:
                                    kind = (0 if mi == 0 else 1) + (2 if (mi == 0 and n == 0) or (mi == 2 and n == 15) else 0)
                                    mk = masks[:, kind, :]
                                    mkb = bass.AP(mk.tensor, mk.offset, [list(mk.ap[0]), [0, 4], list(mk.ap[-1])])
                                    P.tt("dve", pt[:].rearrange("p (g q) -> p g q", g=4), pt[:].rearrange("p (g q) -> p g q", g=4),
                                         mkb, ALU.mult, reads=[MK], writes=[PT])
                                pts.append((pt, PT))
                            ob, OB = obank.get()
                            for g in range(4):
                                for mi in range(3):
                                    bt = n + mi
                                    pt, PT = pts[mi]
                                    P.mm(ob[:, g * 65:(g + 1) * 65], pt[:, g * 128:(g + 1) * 128], v_sb[:, bt, h, :],
                                         mi == 0, mi == 2, reads=[PT, V[bt], VONES], writes=[OB], sig=(g == 3 and mi == 2))
                            sm_, SM = small.get()
                            o3 = ob[:, 0:260].rearrange("p (g d) -> p g d", g=4)
                            P.tt("dve", sm_[:, 0:4], o3[:, :, 64], esink[:, h * 4:(h + 1) * 4], ALU.add,
                                 reads=[OB, ES], writes=[SM])
                            P.recip(sm_[:, 4:8], sm_[:, 0:4], writes=[SM])
                            rd = sm_[:, 4:8]
                            rdb = bass.AP(rd.tensor, rd.offset, [list(rd.ap[0]), list(rd.ap[-1]), [0, 64]])
                            P.tt("dve", ya[:, h * 256:(h + 1) * 256].rearrange("p (g d) -> p g d", g=4), o3[:, :, 0:64], rdb,
                                 ALU.mult, reads=[OB, SM], writes=[YA])
                        groupnorm_T(ya[:], YA, 512, 256, 2, n)

                    for tg in range(4):
                        sl = slice(tg * 512, (tg + 1) * 512)
                        for hm in range(4):
                            hp = slice(64 * (hm % 2), 64 * (hm % 2) + 64)
                            for mc in range(2):
                                bank, BK = sbank.get()
                                P.mm(bank[:], kmT[hp, hm // 2, mc * 128:(mc + 1) * 128], zmT[hp, hm // 2, sl], True, True,
                                     reads=[KM, ZM[hm // 2][tg]], writes=[BK], sig=True)
                                P.act(pmT[hm][mc][:], bank[:], AF.Exp, reads=[BK], writes=[PM[hm][mc]], scale=0.125)
                        for t4 in range(4):
                            tile = tg * 4 + t4
                            ob, OB = obank.get()
                            for hm in range(4):
                                for mc in range(2):
                                    P.mm(ob[:, hm * 65:(hm + 1) * 65], pmT[hm][mc][:, t4 * 128:(t4 + 1) * 128], vm[:, mc, hm, :],
                                         mc == 0, mc == 1, reads=[PM[hm][mc], VM], writes=[OB], sig=(hm == 3 and mc == 1))
                            sm_, SM = small.get()
                            o3 = ob[:, 0:260].rearrange("p (g d) -> p g d", g=4)
                            P.recip(sm_[:, 4:8], o3[:, :, 64], reads=[OB], writes=[SM])
                            rd = sm_[:, 4:8]
                            rdb = bass.AP(rd.tensor, rd.offset, [list(rd.ap[0]), list(rd.ap[-1]), [0, 64]])
                            ya, YA = ya_ring.get()
                            P.tt("dve", ya[:, 0:256].rearrange("p (g d) -> p g d", g=4), o3[:, :, 0:64], rdb,
                                 ALU.mult, reads=[OB, SM], writes=[YA])
                            groupnorm_T(ya[:, 0:256], YA, 256, 768, 6, tile)
                    P.barrier()

                with contextlib.ExitStack() as s3:
                    pq = sbt(s3, "pq", [128, 32, 512], BF16)
                    PQB = Buf()
                    fring = Ring([sbt(s3, f"fr{i}", [128, 4, 512], BF16) for i in range(4)])
                    specT = sbt(s3, "specT", [128, 2, NT], BF16)
                    SP_ = [[Buf() for _ in range(4)] for _ in range(2)]
                    wfbd = sbt(s3, "wfbd", [128, 2, 128], BF16)
                    WF = Buf()
                    ggb = sbt(s3, "ggb3", [128, 256], F32)
                    GG = Buf()
                    junk = sbt(s3, "junk3", [128, 256], BF16)
                    JK = Buf()
                    small = Ring([sbt(s3, f"small3_{i}", [128, 16], F32) for i in range(4)])
                    yan_ring = Ring([sbt(s3, f"yfn{i}", [128, 256], BF16) for i in range(2)])
                    dbank = Ring([pst(s3, f"db{i}", [128, 512], F32) for i in range(4)])
                    ybank = Ring([pst(s3, f"yb{i}", [128, 512], F32) for i in range(2)])
                    tbank = Ring([pst(s3, f"tb3_{i}", [128, 1024], BF16) for i in range(2)])

                    for i in range(4):
                        P.dma("sp", pq[:, i * 8:(i + 1) * 8, :],
                              pq_all[i * 1024:(i + 1) * 1024, :].rearrange("(s p) n -> p s n", p=128), writes=[PQB])
                    P.dma("sp", ggb[:], bcast(ggrp, 256), writes=[GG])
                    P.memset("dve", wfbd[:], 0.0, writes=[WF])
                    for g in range(4):
                        r0 = (g % 2) * 64
                        P.dma("pool", wfbd[r0:r0 + 64, g // 2, r0:r0 + 64], wf[g], writes=[WF])

                    fl = [(kg, sg, cs) for kg in range(4) for sg in range(8) for cs in range(2)]
                    fload = {}
                    fn_ = [0]

                    def fneed(k):
                        while fn_[0] < len(fl) and fn_[0] <= k + 3:
                            kg, sg, cs = fl[fn_[0]]
                            t, B = fring.get()
                            P.dma("sp", t[:], fm[cs, kg, sg].rearrange("p (s n) -> p s n", s=4), writes=[B])
                            fload[fn_[0]] = (t, B)
                            fn_[0] += 1
                        return fload[k]

                    fi = 0
                    for kg in range(4):
                        banks = [dbank.get() for _ in range(2)]
                        for sg in range(8):
                            for cs in range(2):
                                ft, FB = fneed(fi); fi += 1
                                for s4 in range(4):
                                    s = sg * 4 + s4
                                    for m in range(2):
                                        first = (sg == 0 and cs == 0 and s4 == 0)
                                        last = (sg == 7 and cs == 1 and s4 == 3)
                                        P.mm(banks[m][0][:], pq[:, s, m * 256 + cs * 128:m * 256 + (cs + 1) * 128], ft[:, s4, :],
                                             first, last, reads=[PQB, FB], writes=[banks[m][1]],
                                             sig=(last or s4 == 3))
                        for m in range(2):
                            P.act(specT[:, m, kg * 512:(kg + 1) * 512], banks[m][0][:], AF.Copy,
                                  reads=[banks[m][1]], writes=[SP_[m][kg]])
                    for tile in range(16):
                        yb, YB = ybank.get()
                        for m in range(2):
                            P.mm(yb[:, m * 128:(m + 1) * 128], specT[:, m, tile * 128:(tile + 1) * 128], wfbd[:, m, :], True, True,
                                 reads=[SP_[m][tile // 4], WF], writes=[YB], sig=(m == 1))
                        sm_, SM = small.get()
                        P.act(junk[:], yb[:, 0:256], AF.Square, reads=[YB], writes=[JK, SM], accum_out=sm_[:, 0:1])
                        P.act(sm_[:, 1:2], sm_[:, 0:1], AF.Sqrt, reads=[CST], writes=[SM], bias=eps_ap, scale=1.0 / 256)
                        P.recip(sm_[:, 1:2], sm_[:, 1:2], writes=[SM])
                        yn, YN = yan_ring.get()
                        P.stt(yn[:], yb[:, 0:256], sm_[:, 1:2], ggb[:], ALU.mult, ALU.mult, reads=[YB, SM, GG], writes=[YN])
                        tb, TB = tbank.get()
                        for j in range(2):
                            P.tr(tb[:, j * 128:(j + 1) * 128], yn[:, j * 128:(j + 1) * 128], ident_bf[:],
                                 reads=[YN, CST], writes=[TB], sig=(j == 1))
                        P.copy("dve", bufA[:, 0:2, tile * 128:(tile + 1) * 128],
                               tb[:, 0:256].rearrange("p (j t) -> p j t", j=2), reads=[TB],
                               writes=[YC[0][tile], YC[1][tile]])
                    P.barrier()

                with contextlib.ExitStack() as s4_:
                    wo = sbt(s4_, "wo", [128, 8, 8, 128], BF16)
                    WO = [Buf() for _ in range(8)]
                    yo_ring = Ring([sbt(s4_, f"yo{i}", [128, 8, 512], F32) for i in range(1)])
                    sqr = Ring([sbt(s4_, f"sq4_{i}", [128, 512], BF16) for i in range(3)])
                    rstd_ring = Ring([sbt(s4_, f"rstd4_{i}", [128, 512], F32) for i in range(2)])
                    gen = Ring([pst(s4_, f"g4_{i}", [128, 512], F32) for i in range(6)])
                    stb = Ring([pst(s4_, f"s4_{i}", [128, 512], F32) for i in range(2)])
                    for dch in range(8):
                        P.dma("pool", wo[:, dch, :, :], w_out_t[dch].rearrange("p (c n) -> p c n", c=8), writes=[WO[dch]])
                    for tg in range(4):
                        sl = slice(tg * 512, (tg + 1) * 512)
                        yo, _ = yo_ring.get()
                        YO = [Buf() for _ in range(8)]
                        sbk, SBK = stb.get()
                        for dch in range(8):
                            bank, BK = gen.get()
                            for c in range(8):
                                P.mm(bank[:], wo[:, dch, c, :], bufA[:, c, sl], c == 0, c == 7,
                                     reads=[WO[dch]] + [YC[c][tg * 4 + t] for t in range(4)], writes=[BK], sig=(c == 7))
                            if tg > 0 and dch == 0:
                                pass
                            P.act(yo[:, dch, :], bank[:], AF.Copy, reads=[BK], writes=[YO[dch]] + ([YOPREV] if False else []))
                            sq, SQ = sqr.get()
                            P.act(sq[:], bank[:], AF.Square, reads=[BK], writes=[SQ])
                            P.mm(sbk[:], ones_bf[:], sq[:], dch == 0, dch == 7, reads=[SQ, CST], writes=[SBK], sig=True)
                        postnorm_update(tg, 1, yo, YO, rstd_ring, sbk, SBK)
                        nxt_guard = Tok(P.sem["dve"], P.cnt["dve"])
                        P.wait("act", nxt_guard)
                    P.barrier()

                sm.close()
                for th in range(2):
                    with contextlib.ExitStack() as s5:
                        actT = sbt(s5, "actT", [128, NF, 1024], BF16)
                        AC = [[Buf() for _ in range(2)] for _ in range(NF)]
                        with contextlib.ExitStack() as s5a:
                            h2 = sbt(s5a, "h2", [128, 8, 1024], BF16)
                            H2 = [[Buf() for _ in range(2)] for _ in range(8)]
                            sq_ring = Ring([sbt(s5a, f"sq5_{i}", [128, 8, 512], BF16) for i in range(1)])
                            rstd_ring = Ring([sbt(s5a, f"rstd5_{i}", [128, 512], F32) for i in range(2)])
                            wfi = Ring([sbt(s5a, f"wfi{i}", [128, 2, 8, 128], BF16) for i in range(3)])
                            sgr = Ring([sbt(s5a, f"sg{i}", [128, 512], F32) for i in range(2)])
                            gen = Ring([pst(s5a, f"g5_{i}", [128, 512], F32) for i in range(6)])
                            stb = Ring([pst(s5a, f"s5_{i}", [128, 512], F32) for i in range(2)])
                            for tl in range(2):
                                tg = th * 2 + tl
                                prenorm(tg, 2, sq_ring, rstd_ring, stb,
                                        lambda c, tl=tl: h2[:, c, tl * 512:(tl + 1) * 512], [H2[c][tl] for c in range(8)])
                            wl = {}
                            wn = [0]

                            def wneed(k):
                                while wn[0] < NF and wn[0] <= k + 2:
                                    t, B = wfi.get()
                                    P.dma("pool", t[:], w_ffn_in_t[wn[0]].rearrange("p (g c n) -> p g c n", g=2, c=8), writes=[B])
                                    wl[wn[0]] = (t, B)
                                    wn[0] += 1
                                return wl[k]

                            for i in range(NF):
                                wt, WB = wneed(i)
                                for tl in range(2):
                                    sl = slice(tl * 512, (tl + 1) * 512)
                                    bg, BG = gen.get()
                                    bu, BU = gen.get()
                                    for gu, (bk, BKK) in enumerate(((bg, BG), (bu, BU))):
                                        for c in range(8):
                                            P.mm(bk[:], wt[:, gu, c, :], h2[:, c, sl], c == 0, c == 7,
                                                 reads=[WB, H2[c][tl]], writes=[BKK], sig=(c == 7))
                                    sg, SG = sgr.get()
                                    P.act(sg[:], bg[:], AF.Silu, reads=[BG], writes=[SG])
                                    P.tt("dve", actT[:, i, sl], sg[:], bu[:], ALU.mult, reads=[SG, BU], writes=[AC[i][tl]])
                            P.barrier()
                        with contextlib.ExitStack() as s5b:
                            yo = sbt(s5b, "yo5", [128, 2, 8, 512], F32)
                            wfo = Ring([sbt(s5b, f"wfo{i}", [128, NF, 128], BF16) for i in range(2)])
                            sqr = Ring([sbt(s5b, f"sq5b_{i}", [128, 512], BF16) for i in range(3)])
                            rstd_ring = Ring([sbt(s5b, f"rstd5b_{i}", [128, 512], F32) for i in range(2)])
                            gen = Ring([pst(s5b, f"g5b_{i}", [128, 512], F32) for i in range(6)])
                            stbs = [pst(s5b, f"s5b_{i}", [128, 512], F32) for i in range(2)]
                            STB = [Buf(), Buf()]
                            YO = [[Buf() for _ in range(8)] for _ in range(2)]
                            wol = {}
                            won = [0]

                            def woneed(k):
                                while won[0] < 8 and won[0] <= k + 1:
                                    t, B = wfo.get()
                                    P.dma("pool", t[:], w_ffn_out_t[won[0]].rearrange("p (i n) -> p i n", i=NF), writes=[B])
                                    wol[won[0]] = (t, B)
                                    won[0] += 1
                                return wol[k]

                            for dch in range(8):
                                wt, WB = woneed(dch)
                                for tl in range(2):
                                    sl = slice(tl * 512, (tl + 1) * 512)
                                    bank, BK = gen.get()
                                    for i in range(NF):
                                        P.mm(bank[:], wt[:, i, :], actT[:, i, sl], i == 0, i == NF - 1,
                                             reads=[WB, AC[i][tl]], writes=[BK], sig=(i == NF - 1))
                                    P.act(yo[:, tl, dch, :], bank[:], AF.Copy, reads=[BK], writes=[YO[tl][dch]])
                                    sq, SQ = sqr.get()
                                    P.act(sq[:], bank[:], AF.Square, reads=[BK], writes=[SQ])
                                    P.mm(stbs[tl][:], ones_bf[:], sq[:], dch == 0, dch == 7, reads=[SQ, CST], writes=[STB[tl]], sig=True)
                            for tl in range(2):
                                postnorm_update(th * 2 + tl, 3, yo[:, tl], YO[tl], rstd_ring, stbs[tl], STB[tl])
                            P.barrier()

                OUT = Buf()
                for c in range(8):
                    P.dma("sp", xT_out[c * 128:(c + 1) * 128, :], x_sb[:, c, :], reads=X[c] + [OUT])
                for t in OUT.r.values():
                    P.wait("sp", t)
                P.barrier()
    return nc


_CACHE = {}


def _consts():
    if "c" in _CACHE:
        return _CACHE["c"]
    c = {}
    p = np.arange(128)
    half = 32
    inv_freq = np.exp(-np.log(10000.0) * np.arange(half, dtype=np.float32) * np.float32(2.0 / 64)).astype(np.float32)
    cst = np.zeros((128, 8), np.float32)
    cst[:, 0] = inv_freq[p % 32]
    cst[:, 1] = np.where((p % 64) < 32, -1.0, 1.0)
    cst[:, 2] = -1.0
    cst[:, 3] = EPS
    c["cst"] = cst
    cc = np.arange(64)
    angc = 2.0 * np.pi * ((cc[:, None] * cc[None, :]) % 64) / 64.0
    Cc = np.cos(angc) / 8.0
    Sc = np.sin(angc) / 8.0
    csm = np.zeros((128, 256), np.float32)
    for g in range(2):
        csm[g * 64:(g + 1) * 64, g * 64:(g + 1) * 64] = Cc
        csm[g * 64:(g + 1) * 64, 128 + g * 64:128 + (g + 1) * 64] = -Sc
    c["cs_mat"] = csm
    c["ident"] = np.eye(128, dtype=np.float32)
    fms = []
    s_idx = np.arange(S)
    for hf in range(2):
        k_idx = hf * NT + np.arange(NT)
        idx = (s_idx[:, None].astype(np.int64) * k_idx[None, :].astype(np.int64)) % S
        angs = 2.0 * np.pi * idx / S
        mats = np.stack([np.cos(angs) / 64.0, np.sin(angs) / 64.0]).astype(np.float32)
        m = mats.reshape(2, 8, 4, 128, 4, 512)
        m = m.transpose(0, 4, 1, 3, 2, 5).reshape(2, 4, 8, 128, 2048)
        fms.append(np.ascontiguousarray(m.astype(NPBF)))
    c["fm"] = fms
    j = np.arange(128)[:, None]
    q = np.arange(128)[None, :]
    prev = (j >= q).astype(np.float32)
    nxt = (j <= q).astype(np.float32)
    c["masks"] = [np.stack([prev, nxt, prev * (1.0 if hf == 1 else 0.0), nxt * (1.0 if hf == 0 else 0.0)]).astype(np.float32)
                  for hf in range(2)]
    _CACHE["c"] = c
    return c


def _in_cols():
    cols = []
    cols += list(range(0, 256))
    def qcol(head, d):
        return 256 + head * 64 + d
    for c in range(4):
        cols += [qcol(c, d) for d in range(64)] + [qcol(4 + c, d) for d in range(64)]
    cols += list(range(768, 896))
    for c in range(4):
        cols += [qcol(c, (d + 32) % 64) for d in range(64)] + [qcol(4 + c, (d + 32) % 64) for d in range(64)]
    cols += [768 + h * 64 + (d + 32) % 64 for h in range(2) for d in range(64)]
    cols += list(range(1024, 1280))
    cols += list(range(896, 1024))
    return np.array(cols)


def _prep_layer(inp, l):
    w = {}
    wi = inp["w_in"][l][:, _in_cols()]
    w["w_in_t"] = np.ascontiguousarray(wi.reshape(8, 128, 15, 128).transpose(2, 1, 0, 3).reshape(15, 128, 1024))
    gs = [inp[k][l].reshape(8, 128).T for k in ("g_pre_mix", "g_post_mix", "g_pre_ffn", "g_post_ffn", "g_mem")]
    w["gains"] = np.ascontiguousarray(np.concatenate(gs, axis=1).astype(np.float32))
    w["w_mem"] = np.ascontiguousarray(inp["w_mem_kv"][l])
    w["wf"] = np.ascontiguousarray(inp["w_fourier"][l])
    w["sink"] = np.ascontiguousarray(inp["sink"][l].reshape(1, 8))
    w["ggrp"] = np.ascontiguousarray(inp["g_grp"][l].reshape(1, D))
    w["w_out_t"] = np.ascontiguousarray(inp["w_out"][l].reshape(8, 128, 8, 128).transpose(2, 1, 0, 3).reshape(8, 128, 1024))
    w["w_ffn_in_t"] = np.ascontiguousarray(
        inp["w_ffn_in"][l].reshape(8, 128, 2, NF, 128).transpose(3, 1, 2, 0, 4).reshape(NF, 128, 2048))
    w["w_ffn_out_t"] = np.ascontiguousarray(
        inp["w_ffn_out"][l].reshape(NF, 128, 8, 128).transpose(2, 1, 0, 3).reshape(8, 128, DFF))
    return w


def _get_prog(mode):
    if mode not in _CACHE:
        _CACHE[mode] = build(mode)
    return _CACHE[mode]


def kernel(**inputs):
    inp = {k: np.asarray(v) for k, v in inputs.items()}
    cst = _consts()
    x = inp["x"].astype(np.float32, copy=False)
    cores = list(range(8))
    xT = [np.ascontiguousarray(x[c // 2, (c % 2) * NT:(c % 2 + 1) * NT, :].T) for c in cores]
    memT = [np.ascontiguousarray(inp["mem"][c // 2].T) for c in cores]
    posc = [np.ascontiguousarray(inp["positions"][c // 2, (c % 2) * NT:(c % 2 + 1) * NT].reshape(1, NT).astype(np.int32)) for c in cores]
    ncA = _get_prog("A")
    ncB = _get_prog("B")
    for l in range(4):
        w = _prep_layer(inp, l)
        base = [dict(xT=xT[c], pos=posc[c], w_in_t=w["w_in_t"], gains=w["gains"], cst=cst["cst"], cs_mat=cst["cs_mat"])
                for c in cores]
        ra = run_bass_kernel_spmd(ncA, base, core_ids=cores).results
        in_b = []
        for c in cores:
            b, hf = c // 2, c % 2
            pq_all = np.ascontiguousarray(np.concatenate([ra[2 * b]["pq_o"], ra[2 * b + 1]["pq_o"]], axis=0))
            kh_in = np.ascontiguousarray(np.concatenate([ra[2 * b]["kh_o"][:, 128:256], ra[2 * b + 1]["kh_o"][:, 0:128]], axis=1))
            vh_in = np.ascontiguousarray(np.stack([ra[2 * b]["vh_o"][1], ra[2 * b + 1]["vh_o"][0]]))
            d = dict(base[c])
            d.update(memT=memT[c], ident=cst["ident"], masks=cst["masks"][hf], pq_all=pq_all, kh_in=kh_in, vh_in=vh_in,
                     fm=cst["fm"][hf], w_mem=w["w_mem"], wf=w["wf"], sink=w["sink"], ggrp=w["ggrp"],
                     w_out_t=w["w_out_t"], w_ffn_in_t=w["w_ffn_in_t"], w_ffn_out_t=w["w_ffn_out_t"])
            in_b.append(d)
        rb = run_bass_kernel_spmd(ncB, in_b, core_ids=cores).results
        xT = [np.ascontiguousarray(rb[c]["xT_out"]) for c in cores]
    out = np.empty((4, S, D), np.float32)
    for c in cores:
        out[c // 2, (c % 2) * NT:(c % 2 + 1) * NT, :] = xT[c].T
    return out
```

```python
import contextlib
import numpy as np
import ml_dtypes
import concourse.bass as bass
import concourse.mybir as mybir
from concourse.bass_utils import run_bass_kernel_spmd

F32 = mybir.dt.float32
BF16 = mybir.dt.bfloat16
I32 = mybir.dt.int32
AF = mybir.ActivationFunctionType
ALU = mybir.AluOpType
NPBF = ml_dtypes.bfloat16

D = 1024
S = 4096
NT = 2048
DFF = 2816
NF = 22
EPS = 1e-6
TWO_PI = 2.0 * np.pi


class Tok:
    __slots__ = ("sem", "val")

    def __init__(self, sem, val):
        self.sem = sem
        self.val = val


class Buf:
    __slots__ = ("w", "r", "slot")

    def __init__(self):
        self.w = {}
        self.r = {}
        self.slot = None


def _merge(d, tok):
    k = id(tok.sem)
    if k not in d or d[k].val < tok.val:
        d[k] = tok


class Slot:
    def __init__(self, prog, name="sl"):
        self.sem = prog.new_sem(name)
        self.cnt = 0

    def tok(self):
        return Tok(self.sem, self.cnt)


class Prog:
    ENG = ("pe", "act", "dve", "pool", "sp")

    def __init__(self, nc, es):
        self.nc = nc
        self.es = es
        self.q = {e: [] for e in self.ENG}
        self.nsem = 0
        self.sem = {e: self.new_sem("prog_" + e) for e in self.ENG}
        self.cnt = {e: 0 for e in self.ENG}
        self.waited = {}
        self.free_slots = []
        self.used_slots = []

    def new_sem(self, name):
        self.nsem += 1
        return self.es.enter_context(self.nc.semaphore(f"{name}_{self.nsem}"))

    def wait(self, eng, tok):
        key = (eng, id(tok.sem))
        if self.waited.get(key, 0) >= tok.val:
            return
        self.waited[key] = tok.val
        sem, val = tok.sem, tok.val
        self.q[eng].append(lambda e: e.wait_ge(sem, val))

    def _deps(self, eng, reads, writes):
        for b in reads:
            for t in b.w.values():
                self.wait(eng, t)
        for b in writes:
            for t in b.w.values():
                self.wait(eng, t)
            for t in b.r.values():
                self.wait(eng, t)

    def _note(self, tok, reads, writes):
        for b in reads:
            _merge(b.r, tok)
        for b in writes:
            b.w = {id(tok.sem): tok}
            b.r = {}

    def op(self, eng, fn, reads=(), writes=(), sig=True):
        self._deps(eng, reads, writes)
        if not sig:
            self.q[eng].append(fn)
            return None
        self.cnt[eng] += 1
        sem = self.sem[eng]
        self.q[eng].append(lambda e: fn(e).then_inc(sem, 1))
        tok = Tok(sem, self.cnt[eng])
        self._note(tok, reads, writes)
        return tok

    def dma(self, eng, out, in_, reads=(), writes=(), slot=None, **kw):
        self._deps(eng, reads, writes)
        if slot is None:
            owner = (list(writes) + list(reads))[0]
            if owner.slot is None:
                owner.slot = self.free_slots.pop() if self.free_slots else Slot(self, "d")
                self.used_slots.append(owner)
            slot = owner.slot
        slot.cnt += 16
        sem = slot.sem
        self.q[eng].append(lambda e: e.dma_start(out=out, in_=in_, **kw).then_inc(sem, 16))
        tok = slot.tok()
        self._note(tok, reads, writes)
        return tok

    def mm(self, out, lhsT, rhs, start, stop, reads=(), writes=(), sig=False):
        return self.op("pe", lambda e: e.matmul(out, lhsT, rhs, start=start, stop=stop), reads, writes, sig)

    def tr(self, out, in_, ident, reads=(), writes=(), sig=True):
        return self.op("pe", lambda e: e.transpose(out, in_, ident), reads, writes, sig)

    def act(self, out, in_, func, reads=(), writes=(), **kw):
        return self.op("act", lambda e: e.activation(out=out, in_=in_, func=func, **kw), reads, writes)

    def tt(self, eng, out, in0, in1, op, reads=(), writes=()):
        return self.op(eng, lambda e: e.tensor_tensor(out=out, in0=in0, in1=in1, op=op), reads, writes)

    def ts(self, eng, out, in0, s1, s2, op0, op1=None, reads=(), writes=()):
        if op1 is None:
            return self.op(eng, lambda e: e.tensor_scalar(out=out, in0=in0, scalar1=s1, scalar2=None, op0=op0), reads, writes)
        return self.op(eng, lambda e: e.tensor_scalar(out=out, in0=in0, scalar1=s1, scalar2=s2, op0=op0, op1=op1), reads, writes)

    def stt(self, out, in0, scalar, in1, op0, op1, reads=(), writes=()):
        return self.op("dve", lambda e: e.scalar_tensor_tensor(out=out, in0=in0, scalar=scalar, in1=in1, op0=op0, op1=op1), reads, writes)

    def copy(self, eng, out, in_, reads=(), writes=()):
        return self.op(eng, lambda e: e.tensor_copy(out=out, in_=in_), reads, writes)

    def recip(self, out, in_, reads=(), writes=()):
        return self.op("dve", lambda e: e.reciprocal(out=out, in_=in_), reads, writes)

    def memset(self, eng, ap, val, writes=()):
        return self.op(eng, lambda e: e.memset(ap, val), (), writes)

    def barrier(self):
        for b in self.used_slots:
            self.free_slots.append(b.slot)
            b.slot = None
        self.used_slots = []
        nc = self.nc
        q = self.q
        self.q = {e: [] for e in self.ENG}
        with nc.Block() as block:
            @block.tensor
            def _(eng):
                for f in q["pe"]:
                    f(eng)

            @block.scalar
            def _(eng):
                for f in q["act"]:
                    f(eng)

            @block.vector
            def _(eng):
                for f in q["dve"]:
                    f(eng)

            @block.gpsimd
            def _(eng):
                for f in q["pool"]:
                    f(eng)

            @block.sync
            def _(eng):
                for f in q["sp"]:
                    f(eng)


class Ring:
    def __init__(self, items):
        self.items = items
        self.bufs = [Buf() for _ in items]
        self.i = 0

    def get(self):
        i = self.i
        self.i = (i + 1) % len(self.items)
        return self.items[i], self.bufs[i]


def build(mode):
    nc = bass.Bass("TRN2", target_bir_lowering=False)

    def din(name, shape, dt):
        return nc.dram_tensor(name, list(shape), dt, kind="ExternalInput").ap()

    def dout(name, shape, dt):
        return nc.dram_tensor(name, list(shape), dt, kind="ExternalOutput").ap()

    NL = 4 if mode == "F" else 1
    LAYERS = list(range(NL))
    xT = din("xT", [D, NT], F32)
    pos = din("pos", [1, NT], I32)
    w_in_t_all = din("w_in_t", [NL, 15, 128, 1024], F32)
    gains_all = din("gains", [NL, 128, 40], F32)
    cst = din("cst", [128, 8], F32)
    cs_mat = din("cs_mat", [128, 256], F32)
    if mode == "A":
        pq_o = dout("pq_o", [NT, 512], BF16)
        kh_o = dout("kh_o", [128, 256], BF16)
        vh_o = dout("vh_o", [2, 128, 128], BF16)
    else:
        memT = din("memT", [D, 256], F32)
        ident_d = din("ident", [128, 128], F32)
        masks_d = din("masks", [4, 128, 128], F32)
        if mode == "B":
            pq_all = din("pq_all", [S, 512], BF16)
            kh_in = din("kh_in", [128, 256], BF16)
            vh_in = din("vh_in", [2, 128, 128], BF16)
        else:
            exo_t = [[nc.dram_tensor(f"exo_{l}_{p}", [1024, 512], BF16) for p in range(3)] for l in range(NL)]
            exg_t = [[nc.dram_tensor(f"exg_{l}_{p}", [2048, 512], BF16) for p in range(3)] for l in range(NL)]
        fm = din("fm", [2, 4, 8, 128, 2048], BF16)
        w_mem_all = din("w_mem", [NL, D, 512], F32)
        wf_all = din("wf", [NL, 4, 64, 64], F32)
        sink_all = din("sink", [NL, 1, 8], F32)
        ggrp_all = din("ggrp", [NL, 1, D], F32)
        w_out_t_all = din("w_out_t", [NL, 8, 128, 1024], F32)
        w_ffn_in_t_all = din("w_ffn_in_t", [NL, NF, 128, 2048], F32)
        w_ffn_out_t_all = din("w_ffn_out_t", [NL, 8, 128, DFF], F32)
        xT_out = dout("xT_out", [D, NT], F32)
    LW = {}

    with contextlib.ExitStack() as es:
        P = Prog(nc, es)

        uid = [0]

        def sbt(st, name, shape, dt):
            uid[0] += 1
            return st.enter_context(nc.sbuf_tensor(f"s{uid[0]}_{name}", list(shape), dt))

        def pst(st, name, shape, dt):
            uid[0] += 1
            return st.enter_context(nc.psum_tensor(f"p{uid[0]}_{name}", list(shape), dt))

        def bcast(ap, n):
            return bass.AP(ap.tensor, ap.offset, [[0, 128], [1, n]])

        x_sb = sbt(es, "x_sb", [128, 8, NT], F32)
        X = [[Buf() for _ in range(4)] for _ in range(8)]
        cosT = sbt(es, "cosT", [128, NT], F32)
        sinT = sbt(es, "sinT", [128, NT], F32)
        ROPE = Buf()
        cst_sb = sbt(es, "cst_sb", [128, 8], F32)
        gains_sb = sbt(es, "gains_sb", [128, 40], F32)
        CST = Buf()
        ones_bf = sbt(es, "ones_bf", [128, 128], BF16)
        cs_bf = sbt(es, "cs_bf", [128, 256], BF16)
        ident_bf = sbt(es, "ident_bf", [128, 128], BF16)
        v_sb = sbt(es, "v_sb", [128, 18, 2, 65], BF16)
        V = [Buf() for _ in range(18)]
        VONES = Buf()

        for c in range(8):
            P.dma("sp", x_sb[:, c, :], xT[c * 128:(c + 1) * 128, :], writes=X[c])
        P.dma("sp", cst_sb[:], cst, writes=[CST])
        P.dma("pool", cs_bf[:], cs_mat, writes=[CST])
        if mode != "A":
            P.dma("pool", ident_bf[:], ident_d, writes=[CST])
        P.memset("dve", ones_bf[:], 1.0 / D, writes=[CST])
        P.memset("dve", v_sb[:, :, :, 64:65], 1.0, writes=[VONES])
        eps_ap = cst_sb[:, 3:4]

        with contextlib.ExitStack() as s0:
            pos_i = sbt(s0, "pos_i", [128, NT], I32)
            ang = sbt(s0, "ang", [128, NT], F32)
            a2 = sbt(s0, "a2", [128, NT], F32)
            n_i = sbt(s0, "n_i", [128, NT], I32)
            n_f = sbt(s0, "n_f", [128, NT], F32)
            B_pos, B_ang, B_a2, B_ni, B_nf = Buf(), Buf(), Buf(), Buf(), Buf()
            P.dma("sp", pos_i[:], bcast(pos, NT), writes=[B_pos])
            P.copy("dve", ang[:], pos_i[:], reads=[B_pos], writes=[B_ang])
            P.ts("dve", ang[:], ang[:], cst_sb[:, 0:1], None, ALU.mult, reads=[CST], writes=[B_ang])
            for tab, shift, scale in ((sinT, 0.0, cst_sb[:, 1:2]), (cosT, 0.5 * np.pi, 1.0)):
                P.ts("dve", a2[:], ang[:], shift, 1.0 / TWO_PI, ALU.add, ALU.mult, reads=[B_ang], writes=[B_a2])
                P.copy("dve", n_i[:], a2[:], reads=[B_a2], writes=[B_ni])
                P.copy("dve", n_f[:], n_i[:], reads=[B_ni], writes=[B_nf])
                P.ts("dve", a2[:], ang[:], shift, None, ALU.add, reads=[B_ang], writes=[B_a2])
                P.stt(a2[:], n_f[:], -TWO_PI, a2[:], ALU.mult, ALU.add, reads=[B_nf], writes=[B_a2])
                P.ts("dve", a2[:], a2[:], -3.1415925, 3.1415925, ALU.max, ALU.min, writes=[B_a2])
                P.act(tab[:], a2[:], AF.Sin, reads=[B_a2, CST], writes=[ROPE], scale=scale)
            P.barrier()

        GAINS = Buf()
        CCS = {}

        def setup_layer(l):
            LW["w_in_t"] = w_in_t_all[l]
            P.dma("sp", gains_sb[:], gains_all[l], writes=[GAINS])
            if mode != "A":
                LW["w_mem"] = w_mem_all[l]
                LW["wf"] = wf_all[l]
                LW["sink"] = sink_all[l]
                LW["ggrp"] = ggrp_all[l]
                LW["w_out_t"] = w_out_t_all[l]
                LW["w_ffn_in_t"] = w_ffn_in_t_all[l]
                LW["w_ffn_out_t"] = w_ffn_out_t_all[l]
            if mode == "F":
                eo = [t.ap() for t in exo_t[l]]
                eg = [t.ap() for t in exg_t[l]]
                LW["pq_o"] = eo[0:2]
                LW["kh_o"] = eo[2][0:64, :].rearrange("a (b c) -> (a b) c", b=2)
                LW["vh_o"] = eo[2][64:128, :].rearrange("(t a) (b c) -> t (a b) c", t=2, b=4)
                LW["exo"] = exo_t[l]
                LW["exg"] = exg_t[l]
                LW["pq_g"] = eg[0:2]
                LW["kh_prev"] = eg[2][0:64, :].rearrange("a (b c) -> (a b) c", b=2)[:, 128:256]
                LW["kh_next"] = eg[2][1024:1088, :].rearrange("a (b c) -> (a b) c", b=2)[:, 0:128]
                LW["vh_prev"] = eg[2][64:128, :].rearrange("(t a) (b c) -> t (a b) c", t=2, b=4)[1]
                LW["vh_next"] = eg[2][1088:1152, :].rearrange("(t a) (b c) -> t (a b) c", t=2, b=4)[0]

        def prenorm(tg, gidx, sq_ring, rstd_ring, st_ring, h_ap_fn, HB):
            sl = slice(tg * 512, (tg + 1) * 512)
            sq, SQ = sq_ring.get()
            SQc = [Buf() for _ in range(8)]
            for c in range(8):
                P.act(sq[:, c, :], x_sb[:, c, sl], AF.Square, reads=[X[c][tg]], writes=[SQc[c]] + ([SQ] if c == 0 else []))
            bank, BK = st_ring.get()
            for c in range(8):
                last = c == 7
                P.mm(bank[:], ones_bf[:], sq[:, c, :], c == 0, last, reads=[SQc[c], CST] + ([SQ] if last else []),
                     writes=[BK], sig=last)
            rstd, RS = rstd_ring.get()
            P.act(rstd[:], bank[:], AF.Sqrt, reads=[BK, CST], writes=[RS], bias=eps_ap, scale=1.0)
            P.recip(rstd[:], rstd[:], writes=[RS])
            for c in range(8):
                P.stt(h_ap_fn(c), x_sb[:, c, sl], gains_sb[:, gidx * 8 + c:gidx * 8 + c + 1], rstd[:],
                      ALU.mult, ALU.mult, reads=[X[c][tg], RS, CST, GAINS], writes=[HB[c]])

        def postnorm_update(tg, gidx, yo, YO, rstd_ring, bank, BK):
            sl = slice(tg * 512, (tg + 1) * 512)
            rstd, RS = rstd_ring.get()
            P.act(rstd[:], bank[:], AF.Sqrt, reads=[BK, CST], writes=[RS], bias=eps_ap, scale=1.0)
            P.recip(rstd[:], rstd[:], writes=[RS])
            for c in range(8):
                P.tt("dve", yo[:, c, :], yo[:, c, :], rstd[:], ALU.mult, reads=[RS], writes=[YO[c]])
                P.stt(x_sb[:, c, sl], yo[:, c, :], gains_sb[:, gidx * 8 + c:gidx * 8 + c + 1], x_sb[:, c, sl],
                      ALU.mult, ALU.add, reads=[YO[c], CST, GAINS], writes=[X[c][tg]])

        for l in LAYERS:
            setup_layer(l)
            with contextlib.ExitStack() as sm:
                bufA = sbt(sm, "bufA", [128, 8, NT], BF16)
                HT = [[Buf() for _ in range(4)] for _ in range(8)]
                YC = [[Buf() for _ in range(16)] for _ in range(8)]
                qT = sbt(sm, "qT", [128, 4, NT], BF16)
                Q = [[Buf() for _ in range(4)] for _ in range(4)]
                kT = sbt(sm, "kT", [128, 18 * 128], BF16)
                KTG = [Buf() for _ in range(4)]
                KH = [Buf(), Buf()]
                zmT = sbt(sm, "zmT", [128, 2, NT], BF16)
                ZM = [[Buf() for _ in range(4)] for _ in range(2)]

                def KB(bt):
                    return KH[0] if bt == 0 else (KH[1] if bt == 17 else KTG[(bt - 1) // 4])

                with contextlib.ExitStack() as s1:
                    sq_ring = Ring([sbt(s1, f"sq{i}", [128, 8, 512], BF16) for i in range(2)])
                    rstd_ring = Ring([sbt(s1, f"rstd{i}", [128, 512], F32) for i in range(2)])
                    wring = Ring([sbt(s1, f"wr{i}", [128, 8, 128], BF16) for i in range(4)])
                    zfT = sbt(s1, "zfT", [128, 2, NT], BF16)
                    ZF = [[Buf() for _ in range(4)] for _ in range(2)]
                    rt1 = Ring([sbt(s1, f"rt1_{i}", [128, 512], F32) for i in range(2)])
                    rt2 = Ring([sbt(s1, f"rt2_{i}", [128, 512], F32) for i in range(2)])
                    pqst = Ring([sbt(s1, f"pqst{i}", [128, 512], BF16) for i in range(2)])
                    gen = Ring([pst(s1, f"g1_{i}", [128, 512], F32) for i in range(6)])
                    stb = Ring([pst(s1, f"s1_{i}", [128, 512], F32) for i in range(2)])

                    for tg in range(4):
                        prenorm(tg, 0, sq_ring, rstd_ring, stb,
                                lambda c, tg=tg: bufA[:, c, tg * 512:(tg + 1) * 512], [HT[c][tg] for c in range(8)])

                    loads = [0, 1, 2, 7, 3, 8, 4, 9, 5, 10, 6, 11, 12, 13, 14]
                    loaded = {}
                    nxt = [0]

                    def need(k):
                        while nxt[0] < len(loads) and nxt[0] <= k + 2:
                            j = nxt[0]
                            t, B = wring.get()
                            P.dma("pool", t[:], LW["w_in_t"][loads[j]].rearrange("p (c n) -> p c n", c=8), writes=[B])
                            loaded[j] = (t, B)
                            nxt[0] += 1
                        return loaded[k]

                    def zmm(wt, WB, tg):
                        bank, BK = gen.get()
                        sl = slice(tg * 512, (tg + 1) * 512)
                        for c in range(8):
                            P.mm(bank[:], wt[:, c, :], bufA[:, c, sl], c == 0, c == 7,
                                 reads=[WB, HT[c][tg]], writes=[BK], sig=(c == 7))
                        return bank, BK

                    li = 0
                    for m in range(2):
                        wt, WB = need(li); li += 1
                        for tg in range(4):
                            bank, BK = zmm(wt, WB, tg)
                            P.act(zfT[:, m, tg * 512:(tg + 1) * 512], bank[:], AF.Copy, reads=[BK], writes=[ZF[m][tg]])
                    for qc in range(5):
                        wt, WB = need(li); wr_, WRB = need(li + 1); li += 2
                        for tg in range(4):
                            sl = slice(tg * 512, (tg + 1) * 512)
                            b1, BK1 = zmm(wt, WB, tg)
                            b2, BK2 = zmm(wr_, WRB, tg)
                            t1, T1 = rt1.get()
                            t2, T2 = rt2.get()
                            P.tt("dve", t1[:], b1[:], cosT[:, sl], ALU.mult, reads=[BK1, ROPE], writes=[T1])
                            P.tt("dve", t2[:], b2[:], sinT[:, sl], ALU.mult, reads=[BK2, ROPE], writes=[T2])
                            if qc < 4:
                                P.tt("dve", qT[:, qc, sl], t1[:], t2[:], ALU.add, reads=[T1, T2], writes=[Q[qc][tg]])
                            else:
                                P.tt("dve", kT[:, 128 + tg * 512:128 + (tg + 1) * 512], t1[:], t2[:], ALU.add,
                                     reads=[T1, T2], writes=[KTG[tg]])
                    for m in range(2):
                        wt, WB = need(li); li += 1
                        for tg in range(4):
                            bank, BK = zmm(wt, WB, tg)
                            P.act(zmT[:, m, tg * 512:(tg + 1) * 512], bank[:], AF.Copy, reads=[BK], writes=[ZM[m][tg]])
                    wt, WB = need(li); li += 1
                    for tt in range(16):
                        bank, BK = gen.get()
                        for c in range(8):
                            P.mm(bank[:, 0:128], bufA[:, c, tt * 128:(tt + 1) * 128], wt[:, c, :], c == 0, c == 7,
                                 reads=[WB, HT[c][tt // 4]], writes=[BK], sig=(c == 7))
                        P.act(v_sb[:, 1 + tt, :, 0:64], bank[:, 0:128].rearrange("p (h d) -> p h d", h=2), AF.Copy,
                              reads=[BK], writes=[V[1 + tt]])
                    if mode in ("A", "F"):
                        if mode == "A":
                            pq_dst = lambda tt: pq_o[tt * 128:(tt + 1) * 128, :]
                            kh_d, vh_d = kh_o, vh_o
                        else:
                            pq_dst = lambda tt: LW["pq_o"][tt // 8][(tt % 8) * 128:(tt % 8 + 1) * 128, :]
                            kh_d, vh_d = LW["kh_o"], LW["vh_o"]
                        for tt in range(16):
                            bank, BK = gen.get()
                            for m in range(2):
                                P.mm(bank[:, m * 256:(m + 1) * 256], zfT[:, m, tt * 128:(tt + 1) * 128], cs_bf[:], True, True,
                                     reads=[ZF[m][tt // 4], CST], writes=[BK], sig=(m == 1))
                            stg, SB = pqst.get()
                            P.act(stg[:], bank[:], AF.Copy, reads=[BK], writes=[SB])
                            P.dma("sp", pq_dst(tt), stg[:], reads=[SB])
                        KO, VO = Buf(), Buf()
                        P.dma("sp", kh_d[:, 0:128], kT[:, 128:256], reads=[KTG[0], KO])
                        P.dma("sp", kh_d[:, 128:256], kT[:, 16 * 128:17 * 128], reads=[KTG[3], KO])
                        P.dma("sp", vh_d[0].rearrange("p (h d) -> p h d", h=2), v_sb[:, 1, :, 0:64], reads=[V[1], VO])
                        P.dma("sp", vh_d[1].rearrange("p (h d) -> p h d", h=2), v_sb[:, 16, :, 0:64], reads=[V[16], VO])
                        weng = "sp" if mode == "A" else "pool"
                        for b_ in list(pqst.bufs) + [KO, VO]:
                            for t in b_.r.values():
                                P.wait(weng, t)
                        if mode == "F":
                            CCS[l] = []
                            for p_ in range(3):
                                sl_ = Slot(P, "cc")
                                sl_.cnt = 1
                                sem_ = sl_.sem
                                ein, eout = LW["exo"][p_], LW["exg"][p_]
                                P.q["pool"].append(lambda e, ein=ein, eout=eout, sem_=sem_: e.collective_compute(
                                    "AllGather", ALU.bypass, replica_groups=[[0, 1], [2, 3], [4, 5], [6, 7]],
                                    ins=[ein.ap().opt()], outs=[eout.ap().opt()]).then_inc(sem_, 1))
                                CCS[l].append(sl_.tok())
                    P.barrier()

                if mode != "A":
                    with contextlib.ExitStack() as s2:
                        memx = sbt(s2, "memx", [128, 8, 256], F32)
                        memh = sbt(s2, "memh", [128, 8, 256], BF16)
                        msq = sbt(s2, "msq", [128, 8, 256], BF16)
                        mrs = sbt(s2, "mrs", [128, 256], F32)
                        wm = sbt(s2, "wm", [128, 8, 512], BF16)
                        kmT = sbt(s2, "kmT", [128, 2, 256], BF16)
                        vm = sbt(s2, "vm", [128, 2, 4, 65], BF16)
                        masks = sbt(s2, "masks", [128, 4, 128], BF16)
                        esink = sbt(s2, "esink", [128, 8], F32)
                        ggb = sbt(s2, "ggb", [128, D], F32)
                        pT = Ring([sbt(s2, f"pT{i}", [128, 512], BF16) for i in range(6)])
                        pmT = [[sbt(s2, f"pmT{h}_{mc}", [128, 512], BF16) for mc in range(2)] for h in range(4)]
                        PM = [[Buf() for _ in range(2)] for _ in range(4)]
                        ya_ring = Ring([sbt(s2, f"ya{i}", [128, 512], F32) for i in range(2)])
                        yan_ring = Ring([sbt(s2, f"yan{i}", [128, 512], BF16) for i in range(2)])
                        junk = sbt(s2, "junk", [128, 512], BF16)
                        JK = Buf()
                        small = Ring([sbt(s2, f"small{i}", [128, 16], F32) for i in range(4)])
                        sbank = Ring([pst(s2, f"sb{i}", [128, 512], F32) for i in range(4)])
                        obank = Ring([pst(s2, f"ob{i}", [128, 512], F32) for i in range(2)])
                        tbank = Ring([pst(s2, f"tb{i}", [128, 1024], BF16) for i in range(2)])
                        MX, MH, MSQ, MRS, WM, KM, VM, MK, ES, GG = (Buf() for _ in range(10))

                        P.dma("sp", memx[:], memT.rearrange("(c p) m -> p c m", p=128), writes=[MX])
                        P.dma("pool", wm[:], LW["w_mem"].rearrange("(c p) n -> p c n", p=128), writes=[WM])
                        P.dma("pool", masks[:], masks_d.rearrange("k p q -> p k q"), writes=[MK])
                        P.dma("sp", esink[:], bcast(LW["sink"], 8), writes=[ES])
                        P.dma("sp", ggb[:], bcast(LW["ggrp"], D), writes=[GG])
                        if mode == "B":
                            khp, khn, vhp, vhn = kh_in[:, 0:128], kh_in[:, 128:256], vh_in[0], vh_in[1]
                        else:
                            khp, khn, vhp, vhn = LW["kh_prev"], LW["kh_next"], LW["vh_prev"], LW["vh_next"]
                            P.wait("sp", CCS[l][2])
                        P.dma("sp", kT[:, 0:128], khp, writes=[KH[0]])
                        P.dma("sp", kT[:, 17 * 128:18 * 128], khn, writes=[KH[1]])
                        P.dma("sp", v_sb[:, 0, :, 0:64], vhp.rearrange("p (h d) -> p h d", h=2), writes=[V[0]])
                        P.dma("sp", v_sb[:, 17, :, 0:64], vhn.rearrange("p (h d) -> p h d", h=2), writes=[V[17]])
                        P.act(esink[:], esink[:], AF.Exp, writes=[ES])
                        P.memset("dve", vm[:, :, :, 64:65], 1.0, writes=[VM])
                        for c in range(8):
                            P.act(msq[:, c, :], memx[:, c, :], AF.Square, reads=[MX], writes=[MSQ])
                        bank, BK = sbank.get()
                        for c in range(8):
                            P.mm(bank[:, 0:256], ones_bf[:], msq[:, c, :], c == 0, c == 7, reads=[MSQ, CST], writes=[BK], sig=(c == 7))
                        P.act(mrs[:], bank[:, 0:256], AF.Sqrt, reads=[BK, CST], writes=[MRS], bias=eps_ap, scale=1.0)
                        P.recip(mrs[:], mrs[:], writes=[MRS])
                        for c in range(8):
                            P.stt(memh[:, c, :], memx[:, c, :], gains_sb[:, 32 + c:33 + c], mrs[:], ALU.mult, ALU.mult,
                                  reads=[MX, MRS, CST, GAINS], writes=[MH])
                        MHr = [MH]
                        for j in range(2):
                            bank, BK = sbank.get()
                            for c in range(8):
                                P.mm(bank[:, 0:256], wm[:, c, j * 128:(j + 1) * 128], memh[:, c, :], c == 0, c == 7,
                                     reads=[WM] + MHr, writes=[BK], sig=(c == 7))
                            P.act(kmT[:, j, :], bank[:, 0:256], AF.Copy, reads=[BK], writes=[KM])
                        for mt in range(2):
                            bank, BK = sbank.get()
                            for c in range(8):
                                P.mm(bank[:, 0:256], memh[:, c, mt * 128:(mt + 1) * 128], wm[:, c, 256:512], c == 0, c == 7,
                                     reads=[WM] + MHr, writes=[BK], sig=(c == 7))
                            P.act(vm[:, mt, :, 0:64], bank[:, 0:256].rearrange("p (h d) -> p h d", h=4), AF.Copy,
                                  reads=[BK], writes=[VM])

                        def groupnorm_T(y_ap, YB, width, col0, chunk0, tile):
                            sm_, SM = small.get()
                            P.act(junk[:, 0:width], y_ap, AF.Square, reads=[YB], writes=[JK, SM], accum_out=sm_[:, 0:1])
                            P.act(sm_[:, 1:2], sm_[:, 0:1], AF.Sqrt, reads=[CST], writes=[SM], bias=eps_ap, scale=1.0 / width)
                            P.recip(sm_[:, 1:2], sm_[:, 1:2], writes=[SM])
                            yn, YN = yan_ring.get()
                            P.stt(yn[:, 0:width], y_ap, sm_[:, 1:2], ggb[:, col0:col0 + width], ALU.mult, ALU.mult,
                                  reads=[YB, SM, GG], writes=[YN])
                            nch = width // 128
                            tb, TB = tbank.get()
                            for j in range(nch):
                                P.tr(tb[:, j * 128:(j + 1) * 128], yn[:, j * 128:(j + 1) * 128], ident_bf[:],
                                     reads=[YN, CST], writes=[TB], sig=(j == nch - 1))
                            P.copy("act" if False else "dve", bufA[:, chunk0:chunk0 + nch, tile * 128:(tile + 1) * 128],
                                   tb[:, 0:width].rearrange("p (j t) -> p j t", j=nch), reads=[TB],
                                   writes=[YC[chunk0 + j][tile] for j in range(nch)])

                        for n in range(16):
                            ya, YA = ya_ring.get()
                            for h in range(2):
                                hp = slice(64 * h, 64 * h + 64)
                                pts = []
                                for mi in range(3):
                                    bt = n + mi
                                    bank, BK = sbank.get()
                                    P.mm(bank[:], kT[hp, bt * 128:(bt + 1) * 128], qT[hp, :, n * 128:(n + 1) * 128], True, True,
                                         reads=[KB(bt)] + [Q[c][n // 4] for c in range(4)], writes=[BK], sig=True)
                                    pt, PT = pT.get()
                                    P.act(pt[:], bank[:], AF.Exp, reads=[BK], writes=[PT], scale=0.125)
                                    if mi != 1:
                                        kind = (0 if mi == 0 else 1) + (2 if (mi == 0 and n == 0) or (mi == 2 and n == 15) else 0)
                                        mk = masks[:, kind, :]
                                        mkb = bass.AP(mk.tensor, mk.offset, [list(mk.ap[0]), [0, 4], list(mk.ap[-1])])
                                        P.tt("dve", pt[:].rearrange("p (g q) -> p g q", g=4), pt[:].rearrange("p (g q) -> p g q", g=4),
                                             mkb, ALU.mult, reads=[MK], writes=[PT])
                                    pts.append((pt, PT))
                                ob, OB = obank.get()
                                for g in range(4):
                                    for mi in range(3):
                                        bt = n + mi
                                        pt, PT = pts[mi]
                                        P.mm(ob[:, g * 65:(g + 1) * 65], pt[:, g * 128:(g + 1) * 128], v_sb[:, bt, h, :],
                                             mi == 0, mi == 2, reads=[PT, V[bt], VONES], writes=[OB], sig=(g == 3 and mi == 2))
                                sm_, SM = small.get()
                                o3 = ob[:, 0:260].rearrange("p (g d) -> p g d", g=4)
                                P.tt("dve", sm_[:, 0:4], o3[:, :, 64], esink[:, h * 4:(h + 1) * 4], ALU.add,
                                     reads=[OB, ES], writes=[SM])
                                P.recip(sm_[:, 4:8], sm_[:, 0:4], writes=[SM])
                                rd = sm_[:, 4:8]
                                rdb = bass.AP(rd.tensor, rd.offset, [list(rd.ap[0]), list(rd.ap[-1]), [0, 64]])
                                P.tt("dve", ya[:, h * 256:(h + 1) * 256].rearrange("p (g d) -> p g d", g=4), o3[:, :, 0:64], rdb,
                                     ALU.mult, reads=[OB, SM], writes=[YA])
                            groupnorm_T(ya[:], YA, 512, 256, 2, n)

                        for tg in range(4):
                            sl = slice(tg * 512, (tg + 1) * 512)
                            for hm in range(4):
                                hp = slice(64 * (hm % 2), 64 * (hm % 2) + 64)
                                for mc in range(2):
                                    bank, BK = sbank.get()
                                    P.mm(bank[:], kmT[hp, hm // 2, mc * 128:(mc + 1) * 128], zmT[hp, hm // 2, sl], True, True,
                                         reads=[KM, ZM[hm // 2][tg]], writes=[BK], sig=True)
                                    P.act(pmT[hm][mc][:], bank[:], AF.Exp, reads=[BK], writes=[PM[hm][mc]], scale=0.125)
                            for t4 in range(4):
                                tile = tg * 4 + t4
                                ob, OB = obank.get()
                                for hm in range(4):
                                    for mc in range(2):
                                        P.mm(ob[:, hm * 65:(hm + 1) * 65], pmT[hm][mc][:, t4 * 128:(t4 + 1) * 128], vm[:, mc, hm, :],
                                             mc == 0, mc == 1, reads=[PM[hm][mc], VM], writes=[OB], sig=(hm == 3 and mc == 1))
                                sm_, SM = small.get()
                                o3 = ob[:, 0:260].rearrange("p (g d) -> p g d", g=4)
                                P.recip(sm_[:, 4:8], o3[:, :, 64], reads=[OB], writes=[SM])
                                rd = sm_[:, 4:8]
                                rdb = bass.AP(rd.tensor, rd.offset, [list(rd.ap[0]), list(rd.ap[-1]), [0, 64]])
                                ya, YA = ya_ring.get()
                                P.tt("dve", ya[:, 0:256].rearrange("p (g d) -> p g d", g=4), o3[:, :, 0:64], rdb,
                                     ALU.mult, reads=[OB, SM], writes=[YA])
                                groupnorm_T(ya[:, 0:256], YA, 256, 768, 6, tile)
                        P.barrier()

                    with contextlib.ExitStack() as s3:
                        pq = sbt(s3, "pq", [128, 32, 512], BF16)
                        PQB = Buf()
                        fring = Ring([sbt(s3, f"fr{i}", [128, 4, 512], BF16) for i in range(4)])
                        specT = sbt(s3, "specT", [128, 2, NT], BF16)
                        SP_ = [[Buf() for _ in range(4)] for _ in range(2)]
                        wfbd = sbt(s3, "wfbd", [128, 2, 128], BF16)
                        WF = Buf()
                        ggb = sbt(s3, "ggb3", [128, 256], F32)
                        GG = Buf()
                        junk = sbt(s3, "junk3", [128, 256], BF16)
                        JK = Buf()
                        small = Ring([sbt(s3, f"small3_{i}", [128, 16], F32) for i in range(4)])
                        yan_ring = Ring([sbt(s3, f"yfn{i}", [128, 256], BF16) for i in range(2)])
                        dbank = Ring([pst(s3, f"db{i}", [128, 512], F32) for i in range(4)])
                        ybank = Ring([pst(s3, f"yb{i}", [128, 512], F32) for i in range(2)])
                        tbank = Ring([pst(s3, f"tb3_{i}", [128, 1024], BF16) for i in range(2)])

                        if mode == "B":
                            for i in range(4):
                                P.dma("sp", pq[:, i * 8:(i + 1) * 8, :],
                                      pq_all[i * 1024:(i + 1) * 1024, :].rearrange("(s p) n -> p s n", p=128), writes=[PQB])
                        else:
                            for p_ in range(2):
                                P.wait("sp", CCS[l][p_])
                            for r in range(2):
                                for p_ in range(2):
                                    P.dma("sp", pq[:, r * 16 + p_ * 8:r * 16 + (p_ + 1) * 8, :],
                                          LW["pq_g"][p_][r * 1024:(r + 1) * 1024, :].rearrange("(s p) n -> p s n", p=128),
                                          writes=[PQB])
                        P.dma("sp", ggb[:], bcast(LW["ggrp"], 256), writes=[GG])
                        P.memset("dve", wfbd[:], 0.0, writes=[WF])
                        for g in range(4):
                            r0 = (g % 2) * 64
                            P.dma("pool", wfbd[r0:r0 + 64, g // 2, r0:r0 + 64], LW["wf"][g], writes=[WF])

                        fl = [(kg, sg, cs) for kg in range(4) for sg in range(8) for cs in range(2)]
                        fload = {}
                        fn_ = [0]

                        def fneed(k):
                            while fn_[0] < len(fl) and fn_[0] <= k + 3:
                                kg, sg, cs = fl[fn_[0]]
                                t, B = fring.get()
                                P.dma("sp", t[:], fm[cs, kg, sg].rearrange("p (s n) -> p s n", s=4), writes=[B])
                                fload[fn_[0]] = (t, B)
                                fn_[0] += 1
                            return fload[k]

                        fi = 0
                        for kg in range(4):
                            banks = [dbank.get() for _ in range(2)]
                            for sg in range(8):
                                for cs in range(2):
                                    ft, FB = fneed(fi); fi += 1
                                    for s4 in range(4):
                                        s = sg * 4 + s4
                                        for m in range(2):
                                            first = (sg == 0 and cs == 0 and s4 == 0)
                                            last = (sg == 7 and cs == 1 and s4 == 3)
                                            P.mm(banks[m][0][:], pq[:, s, m * 256 + cs * 128:m * 256 + (cs + 1) * 128], ft[:, s4, :],
                                                 first, last, reads=[PQB, FB], writes=[banks[m][1]],
                                                 sig=(last or s4 == 3))
                            for m in range(2):
                                P.act(specT[:, m, kg * 512:(kg + 1) * 512], banks[m][0][:], AF.Copy,
                                      reads=[banks[m][1]], writes=[SP_[m][kg]])
                        for tile in range(16):
                            yb, YB = ybank.get()
                            for m in range(2):
                                P.mm(yb[:, m * 128:(m + 1) * 128], specT[:, m, tile * 128:(tile + 1) * 128], wfbd[:, m, :], True, True,
                                     reads=[SP_[m][tile // 4], WF], writes=[YB], sig=(m == 1))
                            sm_, SM = small.get()
                            P.act(junk[:], yb[:, 0:256], AF.Square, reads=[YB], writes=[JK, SM], accum_out=sm_[:, 0:1])
                            P.act(sm_[:, 1:2], sm_[:, 0:1], AF.Sqrt, reads=[CST], writes=[SM], bias=eps_ap, scale=1.0 / 256)
                            P.recip(sm_[:, 1:2], sm_[:, 1:2], writes=[SM])
                            yn, YN = yan_ring.get()
                            P.stt(yn[:], yb[:, 0:256], sm_[:, 1:2], ggb[:], ALU.mult, ALU.mult, reads=[YB, SM, GG], writes=[YN])
                            tb, TB = tbank.get()
                            for j in range(2):
                                P.tr(tb[:, j * 128:(j + 1) * 128], yn[:, j * 128:(j + 1) * 128], ident_bf[:],
                                     reads=[YN, CST], writes=[TB], sig=(j == 1))
                            P.copy("dve", bufA[:, 0:2, tile * 128:(tile + 1) * 128],
                                   tb[:, 0:256].rearrange("p (j t) -> p j t", j=2), reads=[TB],
                                   writes=[YC[0][tile], YC[1][tile]])
                        P.barrier()

                    with contextlib.ExitStack() as s4_:
                        wo = sbt(s4_, "wo", [128, 8, 8, 128], BF16)
                        WO = [Buf() for _ in range(8)]
                        yo_ring = Ring([sbt(s4_, f"yo{i}", [128, 8, 512], F32) for i in range(1)])
                        sqr = Ring([sbt(s4_, f"sq4_{i}", [128, 512], BF16) for i in range(3)])
                        rstd_ring = Ring([sbt(s4_, f"rstd4_{i}", [128, 512], F32) for i in range(2)])
                        gen = Ring([pst(s4_, f"g4_{i}", [128, 512], F32) for i in range(6)])
                        stb = Ring([pst(s4_, f"s4_{i}", [128, 512], F32) for i in range(2)])
                        for dch in range(8):
                            P.dma("pool", wo[:, dch, :, :], LW["w_out_t"][dch].rearrange("p (c n) -> p c n", c=8), writes=[WO[dch]])
                        for tg in range(4):
                            sl = slice(tg * 512, (tg + 1) * 512)
                            yo, _ = yo_ring.get()
                            YO = [Buf() for _ in range(8)]
                            sbk, SBK = stb.get()
                            for dch in range(8):
                                bank, BK = gen.get()
                                for c in range(8):
                                    P.mm(bank[:], wo[:, dch, c, :], bufA[:, c, sl], c == 0, c == 7,
                                         reads=[WO[dch]] + [YC[c][tg * 4 + t] for t in range(4)], writes=[BK], sig=(c == 7))
                                if tg > 0 and dch == 0:
                                    pass
                                P.act(yo[:, dch, :], bank[:], AF.Copy, reads=[BK], writes=[YO[dch]] + ([YOPREV] if False else []))
                                sq, SQ = sqr.get()
                                P.act(sq[:], bank[:], AF.Square, reads=[BK], writes=[SQ])
                                P.mm(sbk[:], ones_bf[:], sq[:], dch == 0, dch == 7, reads=[SQ, CST], writes=[SBK], sig=True)
                            postnorm_update(tg, 1, yo, YO, rstd_ring, sbk, SBK)
                            nxt_guard = Tok(P.sem["dve"], P.cnt["dve"])
                            P.wait("act", nxt_guard)
                        P.barrier()

                    sm.close()
                    for th in range(2):
                        with contextlib.ExitStack() as s5:
                            actT = sbt(s5, "actT", [128, NF, 1024], BF16)
                            AC = [[Buf() for _ in range(2)] for _ in range(NF)]
                            with contextlib.ExitStack() as s5a:
                                h2 = sbt(s5a, "h2", [128, 8, 1024], BF16)
                                H2 = [[Buf() for _ in range(2)] for _ in range(8)]
                                sq_ring = Ring([sbt(s5a, f"sq5_{i}", [128, 8, 512], BF16) for i in range(1)])
                                rstd_ring = Ring([sbt(s5a, f"rstd5_{i}", [128, 512], F32) for i in range(2)])
                                wfi = Ring([sbt(s5a, f"wfi{i}", [128, 2, 8, 128], BF16) for i in range(3)])
                                sgr = Ring([sbt(s5a, f"sg{i}", [128, 512], F32) for i in range(2)])
                                gen = Ring([pst(s5a, f"g5_{i}", [128, 512], F32) for i in range(6)])
                                stb = Ring([pst(s5a, f"s5_{i}", [128, 512], F32) for i in range(2)])
                                for tl in range(2):
                                    tg = th * 2 + tl
                                    prenorm(tg, 2, sq_ring, rstd_ring, stb,
                                            lambda c, tl=tl: h2[:, c, tl * 512:(tl + 1) * 512], [H2[c][tl] for c in range(8)])
                                wl = {}
                                wn = [0]

                                def wneed(k):
                                    while wn[0] < NF and wn[0] <= k + 2:
                                        t, B = wfi.get()
                                        P.dma("pool", t[:], LW["w_ffn_in_t"][wn[0]].rearrange("p (g c n) -> p g c n", g=2, c=8), writes=[B])
                                        wl[wn[0]] = (t, B)
                                        wn[0] += 1
                                    return wl[k]

                                for i in range(NF):
                                    wt, WB = wneed(i)
                                    for tl in range(2):
                                        sl = slice(tl * 512, (tl + 1) * 512)
                                        bg, BG = gen.get()
                                        bu, BU = gen.get()
                                        for gu, (bk, BKK) in enumerate(((bg, BG), (bu, BU))):
                                            for c in range(8):
                                                P.mm(bk[:], wt[:, gu, c, :], h2[:, c, sl], c == 0, c == 7,
                                                     reads=[WB, H2[c][tl]], writes=[BKK], sig=(c == 7))
                                        sg, SG = sgr.get()
                                        P.act(sg[:], bg[:], AF.Silu, reads=[BG], writes=[SG])
                                        P.tt("dve", actT[:, i, sl], sg[:], bu[:], ALU.mult, reads=[SG, BU], writes=[AC[i][tl]])
                                P.barrier()
                            with contextlib.ExitStack() as s5b:
                                yo = sbt(s5b, "yo5", [128, 2, 8, 512], F32)
                                wfo = Ring([sbt(s5b, f"wfo{i}", [128, NF, 128], BF16) for i in range(2)])
                                sqr = Ring([sbt(s5b, f"sq5b_{i}", [128, 512], BF16) for i in range(3)])
                                rstd_ring = Ring([sbt(s5b, f"rstd5b_{i}", [128, 512], F32) for i in range(2)])
                                gen = Ring([pst(s5b, f"g5b_{i}", [128, 512], F32) for i in range(6)])
                                stbs = [pst(s5b, f"s5b_{i}", [128, 512], F32) for i in range(2)]
                                STB = [Buf(), Buf()]
                                YO = [[Buf() for _ in range(8)] for _ in range(2)]
                                wol = {}
                                won = [0]

                                def woneed(k):
                                    while won[0] < 8 and won[0] <= k + 1:
                                        t, B = wfo.get()
                                        P.dma("pool", t[:], LW["w_ffn_out_t"][won[0]].rearrange("p (i n) -> p i n", i=NF), writes=[B])
                                        wol[won[0]] = (t, B)
                                        won[0] += 1
                                    return wol[k]

                                for dch in range(8):
                                    wt, WB = woneed(dch)
                                    for tl in range(2):
                                        sl = slice(tl * 512, (tl + 1) * 512)
                                        bank, BK = gen.get()
                                        for i in range(NF):
                                            P.mm(bank[:], wt[:, i, :], actT[:, i, sl], i == 0, i == NF - 1,
                                                 reads=[WB, AC[i][tl]], writes=[BK], sig=(i == NF - 1))
                                        P.act(yo[:, tl, dch, :], bank[:], AF.Copy, reads=[BK], writes=[YO[tl][dch]])
                                        sq, SQ = sqr.get()
                                        P.act(sq[:], bank[:], AF.Square, reads=[BK], writes=[SQ])
                                        P.mm(stbs[tl][:], ones_bf[:], sq[:], dch == 0, dch == 7, reads=[SQ, CST], writes=[STB[tl]], sig=True)
                                for tl in range(2):
                                    postnorm_update(th * 2 + tl, 3, yo[:, tl], YO[tl], rstd_ring, stbs[tl], STB[tl])
                                P.barrier()

        if mode != "A":
            OUT = Buf()
            for c in range(8):
                P.dma("sp", xT_out[c * 128:(c + 1) * 128, :], x_sb[:, c, :], reads=X[c] + [OUT])
            for t in OUT.r.values():
                P.wait("sp", t)
            P.barrier()
    return nc


_CACHE = {}


def _consts():
    if "c" in _CACHE:
        return _CACHE["c"]
    c = {}
    p = np.arange(128)
    half = 32
    inv_freq = np.exp(-np.log(10000.0) * np.arange(half, dtype=np.float32) * np.float32(2.0 / 64)).astype(np.float32)
    cst = np.zeros((128, 8), np.float32)
    cst[:, 0] = inv_freq[p % 32]
    cst[:, 1] = np.where((p % 64) < 32, -1.0, 1.0)
    cst[:, 2] = -1.0
    cst[:, 3] = EPS
    c["cst"] = cst
    cc = np.arange(64)
    angc = 2.0 * np.pi * ((cc[:, None] * cc[None, :]) % 64) / 64.0
    Cc = np.cos(angc) / 8.0
    Sc = np.sin(angc) / 8.0
    csm = np.zeros((128, 256), np.float32)
    for g in range(2):
        csm[g * 64:(g + 1) * 64, g * 64:(g + 1) * 64] = Cc
        csm[g * 64:(g + 1) * 64, 128 + g * 64:128 + (g + 1) * 64] = -Sc
    c["cs_mat"] = csm
    c["ident"] = np.eye(128, dtype=np.float32)
    fms = []
    s_idx = np.arange(S)
    for hf in range(2):
        k_idx = hf * NT + np.arange(NT)
        idx = (s_idx[:, None].astype(np.int64) * k_idx[None, :].astype(np.int64)) % S
        angs = 2.0 * np.pi * idx / S
        mats = np.stack([np.cos(angs) / 64.0, np.sin(angs) / 64.0]).astype(np.float32)
        m = mats.reshape(2, 8, 4, 128, 4, 512)
        m = m.transpose(0, 4, 1, 3, 2, 5).reshape(2, 4, 8, 128, 2048)
        fms.append(np.ascontiguousarray(m.astype(NPBF)))
    c["fm"] = fms
    j = np.arange(128)[:, None]
    q = np.arange(128)[None, :]
    prev = (j >= q).astype(np.float32)
    nxt = (j <= q).astype(np.float32)
    c["masks"] = [np.stack([prev, nxt, prev * (1.0 if hf == 1 else 0.0), nxt * (1.0 if hf == 0 else 0.0)]).astype(np.float32)
                  for hf in range(2)]
    _CACHE["c"] = c
    return c


def _in_cols():
    cols = []
    cols += list(range(0, 256))
    def qcol(head, d):
        return 256 + head * 64 + d
    for c in range(4):
        cols += [qcol(c, d) for d in range(64)] + [qcol(4 + c, d) for d in range(64)]
    cols += list(range(768, 896))
    for c in range(4):
        cols += [qcol(c, (d + 32) % 64) for d in range(64)] + [qcol(4 + c, (d + 32) % 64) for d in range(64)]
    cols += [768 + h * 64 + (d + 32) % 64 for h in range(2) for d in range(64)]
    cols += list(range(1024, 1280))
    cols += list(range(896, 1024))
    return np.array(cols)


def _prep_layer(inp, l):
    w = {}
    wi = inp["w_in"][l][:, _in_cols()]
    w["w_in_t"] = np.ascontiguousarray(wi.reshape(8, 128, 15, 128).transpose(2, 1, 0, 3).reshape(15, 128, 1024))
    gs = [inp[k][l].reshape(8, 128).T for k in ("g_pre_mix", "g_post_mix", "g_pre_ffn", "g_post_ffn", "g_mem")]
    w["gains"] = np.ascontiguousarray(np.concatenate(gs, axis=1).astype(np.float32))
    w["w_mem"] = np.ascontiguousarray(inp["w_mem_kv"][l])
    w["wf"] = np.ascontiguousarray(inp["w_fourier"][l])
    w["sink"] = np.ascontiguousarray(inp["sink"][l].reshape(1, 8))
    w["ggrp"] = np.ascontiguousarray(inp["g_grp"][l].reshape(1, D))
    w["w_out_t"] = np.ascontiguousarray(inp["w_out"][l].reshape(8, 128, 8, 128).transpose(2, 1, 0, 3).reshape(8, 128, 1024))
    w["w_ffn_in_t"] = np.ascontiguousarray(
        inp["w_ffn_in"][l].reshape(8, 128, 2, NF, 128).transpose(3, 1, 2, 0, 4).reshape(NF, 128, 2048))
    w["w_ffn_out_t"] = np.ascontiguousarray(
        inp["w_ffn_out"][l].reshape(NF, 128, 8, 128).transpose(2, 1, 0, 3).reshape(8, 128, DFF))
    return w


def _get_prog(mode):
    if mode not in _CACHE:
        _CACHE[mode] = build(mode)
    return _CACHE[mode]


def kernel(**inputs):
    inp = {k: np.asarray(v) for k, v in inputs.items()}
    cst = _consts()
    x = inp["x"].astype(np.float32, copy=False)
    cores = list(range(8))
    ws = [_prep_layer(inp, l) for l in range(4)]
    W = {k: np.ascontiguousarray(np.stack([w[k] for w in ws])) for k in ws[0]}
    ncF = _get_prog("F")
    in_maps = []
    for c in cores:
        b, hf = c // 2, c % 2
        d = dict(
            xT=np.ascontiguousarray(x[b, hf * NT:(hf + 1) * NT, :].T),
            pos=np.ascontiguousarray(inp["positions"][b, hf * NT:(hf + 1) * NT].reshape(1, NT).astype(np.int32)),
            memT=np.ascontiguousarray(inp["mem"][b].T),
            cst=cst["cst"], cs_mat=cst["cs_mat"], ident=cst["ident"], masks=cst["masks"][hf], fm=cst["fm"][hf],
        )
        d.update(W)
        in_maps.append(d)
    res = run_bass_kernel_spmd(ncF, in_maps, core_ids=cores).results
    out = np.empty((4, S, D), np.float32)
    for c in cores:
        out[c // 2, (c % 2) * NT:(c % 2 + 1) * NT, :] = np.asarray(res[c]["xT_out"]).T
    return out
```
